# Optimizing a Trainium2 kernel written in Bass

```python
import math
import jax, jax.numpy as jnp
from jax import lax
import numpy as np

D_MODEL = 1024
BATCH = 2
SEQ = 8192
DEPTH = 4

ATTN_HEADS = 16
ATTN_KV_HEADS = 4
ATTN_GROUP = ATTN_HEADS // ATTN_KV_HEADS
ATTN_HEAD_DIM = 64
ATTN_Q_DIM = ATTN_HEADS * ATTN_HEAD_DIM
ATTN_KV_DIM = ATTN_KV_HEADS * ATTN_HEAD_DIM
ATTN_IN_DIM = ATTN_Q_DIM + 2 * ATTN_KV_DIM
WINDOW = 128
ATTN_BLOCK = 128
ROPE_DIM = ATTN_HEAD_DIM // 4
ROPE_THETA = 500000.0

GDN_QK_HEADS = 8
GDN_V_HEADS = 16
GDN_HEAD_K = 128
GDN_HEAD_V = 128
GDN_CONV = 4
GDN_CHUNK = 64
GDN_KEY_DIM = GDN_QK_HEADS * GDN_HEAD_K
GDN_VAL_DIM = GDN_V_HEADS * GDN_HEAD_V
GDN_QKV_DIM = 2 * GDN_KEY_DIM + GDN_VAL_DIM
GDN_IN_DIM = GDN_QKV_DIM + GDN_VAL_DIM + 2 * GDN_V_HEADS

D_FF = 3 * D_MODEL
FFN_CONV = 3

N_ATTN_LAYERS = (DEPTH + 1) // 2
N_GDN_LAYERS = DEPTH // 2
EPS = 1e-6

kernel_name = "hybrid_swa_sink_gdn_convffn"


def rms_norm(x, gain):
    xf = x.astype(jnp.float32)
    y = xf * lax.rsqrt(jnp.mean(xf * xf, axis=-1, keepdims=True) + EPS)
    return (y * gain.astype(jnp.float32)).astype(x.dtype)


def l2_norm(x):
    return x * lax.rsqrt(jnp.sum(x * x, axis=-1, keepdims=True) + EPS)


def causal_dwconv(x, w):
    k = w.shape[0]
    return lax.conv_general_dilated(
        x, w[:, None, :].astype(x.dtype), window_strides=(1,), padding=[(k - 1, 0)],
        dimension_numbers=("NWC", "WIO", "NWC"), feature_group_count=x.shape[-1])


def rope_tables(positions):
    inv_freq = ROPE_THETA ** (-jnp.arange(0, ROPE_DIM, 2, dtype=jnp.float32) / ROPE_DIM)
    ang = positions.astype(jnp.float32)[..., None] * inv_freq
    return jnp.cos(ang)[:, :, None, :], jnp.sin(ang)[:, :, None, :]


def partial_rope(x, cos, sin):
    cos = cos.astype(x.dtype)
    sin = sin.astype(x.dtype)
    half = ROPE_DIM // 2
    x1, x2, xp = x[..., :half], x[..., half:ROPE_DIM], x[..., ROPE_DIM:]
    return jnp.concatenate([x1 * cos - x2 * sin, x2 * cos + x1 * sin, xp], axis=-1)


def swa_sink_attention(u, w_in, q_gain, k_gain, sinks, w_out, cos, sin):
    B, S, _ = u.shape
    nb = S // ATTN_BLOCK
    blk, hkv, g, hd = ATTN_BLOCK, ATTN_KV_HEADS, ATTN_GROUP, ATTN_HEAD_DIM
    qkv = u @ w_in
    q = qkv[..., :ATTN_Q_DIM].reshape(B, S, ATTN_HEADS, hd)
    k = qkv[..., ATTN_Q_DIM:ATTN_Q_DIM + ATTN_KV_DIM].reshape(B, S, hkv, hd)
    v = qkv[..., ATTN_Q_DIM + ATTN_KV_DIM:].reshape(B, S, hkv, hd)
    q = partial_rope(rms_norm(q, q_gain), cos, sin)
    k = partial_rope(rms_norm(k, k_gain), cos, sin)
    qb = q.reshape(B, nb, blk, hkv, g, hd)

    def band(t):
        prev = jnp.pad(t, ((0, 0), (blk, 0), (0, 0), (0, 0)))[:, :S]
        return jnp.concatenate([prev.reshape(B, nb, blk, hkv, hd),
                                t.reshape(B, nb, blk, hkv, hd)], axis=2)

    kb, vb = band(k), band(v)
    scores = jnp.einsum("bnqhgd,bnkhd->bnhgqk", qb, kb,
                        preferred_element_type=jnp.float32) * (hd ** -0.5)
    dist = jnp.arange(blk)[:, None] + blk - jnp.arange(2 * blk)[None, :]
    in_window = (dist >= 0) & (dist < WINDOW)
    key_pos = jnp.arange(nb)[:, None] * blk - blk + jnp.arange(2 * blk)[None, :]
    mask = in_window[None] & (key_pos >= 0)[:, None, :]
    scores = jnp.where(mask[None, :, None, None], scores, jnp.finfo(jnp.float32).min)
    sink = jnp.broadcast_to(sinks.astype(jnp.float32).reshape(1, 1, hkv, g, 1, 1),
                            scores.shape[:-1] + (1,))
    probs = jax.nn.softmax(jnp.concatenate([scores, sink], axis=-1), axis=-1)[..., :-1]
    out = jnp.einsum("bnhgqk,bnkhd->bnqhgd", probs.astype(vb.dtype), vb)
    return out.reshape(B, S, ATTN_Q_DIM) @ w_out


def chunk_gated_delta_rule(q, k, v, g, beta):
    out_dtype = v.dtype
    B, S, H, DK = q.shape
    DV = v.shape[-1]
    C = GDN_CHUNK
    nc = S // C
    f32 = jnp.float32

    def chunks(t):
        t = t.astype(f32).reshape((B, nc, C, H) + t.shape[3:])
        return jnp.moveaxis(t, 3, 1)

    q, k, v, g, beta = chunks(q), chunks(k), chunks(v), chunks(g), chunks(beta)
    gc = jnp.cumsum(g, axis=-1)
    incl = jnp.tril(jnp.ones((C, C), dtype=bool))
    strict = jnp.tril(jnp.ones((C, C), dtype=bool), k=-1)
    decay = jnp.exp(jnp.where(incl, gc[..., :, None] - gc[..., None, :], -jnp.inf))
    kk = jnp.einsum("bhncd,bhnsd->bhncs", k, k)
    m = jnp.where(strict, beta[..., :, None] * kk * decay, 0.0)
    a = m + jnp.eye(C, dtype=f32)
    rhs = jnp.concatenate([v * beta[..., None], k * (beta * jnp.exp(gc))[..., None]], axis=-1)
    sol = lax.linalg.triangular_solve(a, rhs, left_side=True, lower=True, unit_diagonal=True)
    u, w = sol[..., :DV], sol[..., DV:]
    qk = jnp.where(incl, jnp.einsum("bhncd,bhnsd->bhncs", q, k) * decay, 0.0)
    q_dec = q * jnp.exp(gc)[..., None]
    k_dec = k * jnp.exp(gc[..., -1:] - gc)[..., None]
    g_last = jnp.exp(gc[..., -1])

    def step(state, inp):
        u_c, w_c, qk_c, qd_c, kd_c, gl_c = inp
        v_new = u_c - jnp.einsum("bhcd,bhde->bhce", w_c, state)
        o = jnp.einsum("bhcd,bhde->bhce", qd_c, state) + jnp.einsum("bhcs,bhse->bhce", qk_c, v_new)
        state = state * gl_c[..., None, None] + jnp.einsum("bhcd,bhce->bhde", kd_c, v_new)
        return state, o

    xs = tuple(jnp.moveaxis(t, 2, 0) for t in (u, w, qk, q_dec, k_dec, g_last))
    _, o = lax.scan(step, jnp.zeros((B, H, DK, DV), f32), xs)
    o = jnp.transpose(o, (1, 0, 3, 2, 4)).reshape(B, S, H, DV)
    return o.astype(out_dtype)


def gated_deltanet(u, w_in, conv_w, a_log, dt_bias, norm_gain, w_out):
    B, S, _ = u.shape
    proj = u @ w_in
    qkv = jax.nn.silu(causal_dwconv(proj[..., :GDN_QKV_DIM], conv_w))
    z = proj[..., GDN_QKV_DIM:GDN_QKV_DIM + GDN_VAL_DIM].reshape(B, S, GDN_V_HEADS, GDN_HEAD_V)
    b = proj[..., GDN_QKV_DIM + GDN_VAL_DIM:GDN_QKV_DIM + GDN_VAL_DIM + GDN_V_HEADS]
    a = proj[..., GDN_QKV_DIM + GDN_VAL_DIM + GDN_V_HEADS:]
    rep = GDN_V_HEADS // GDN_QK_HEADS
    q = qkv[..., :GDN_KEY_DIM].reshape(B, S, GDN_QK_HEADS, GDN_HEAD_K).astype(jnp.float32)
    k = qkv[..., GDN_KEY_DIM:2 * GDN_KEY_DIM].reshape(B, S, GDN_QK_HEADS, GDN_HEAD_K).astype(jnp.float32)
    v = qkv[..., 2 * GDN_KEY_DIM:].reshape(B, S, GDN_V_HEADS, GDN_HEAD_V)
    q = jnp.repeat(l2_norm(q) * (GDN_HEAD_K ** -0.5), rep, axis=2)
    k = jnp.repeat(l2_norm(k), rep, axis=2)
    beta = jax.nn.sigmoid(b.astype(jnp.float32))
    g = -jnp.exp(a_log.astype(jnp.float32)) * jax.nn.softplus(
        a.astype(jnp.float32) + dt_bias.astype(jnp.float32))
    o = chunk_gated_delta_rule(q, k, v, g, beta)
    o = rms_norm(o, norm_gain) * jax.nn.silu(z)
    return o.reshape(B, S, GDN_VAL_DIM) @ w_out


def conv_glu_ffn(u, w_up, conv_w, conv_b, w_down):
    gu = u @ w_up
    gate = causal_dwconv(gu[..., :D_FF], conv_w) + conv_b
    return (jax.nn.silu(gate) * gu[..., D_FF:]) @ w_down


def setup_inputs(seed: int = 0) -> dict:
    key = jax.random.key(seed)
    ks = jax.random.split(key, 24)
    f32 = jnp.float32

    def nrm(k, shape, scale):
        return jax.random.normal(k, shape, f32) * scale

    x = nrm(ks[0], (BATCH, SEQ, D_MODEL), 1.0)
    start = jax.random.randint(ks[1], (BATCH, 1), 0, 4096, dtype=jnp.int32)
    positions = start + jnp.arange(SEQ, dtype=jnp.int32)[None, :]
    mixer_norm = 1.0 + nrm(ks[2], (DEPTH, D_MODEL), 0.05)
    ffn_norm = 1.0 + nrm(ks[3], (DEPTH, D_MODEL), 0.05)

    attn_w_in = nrm(ks[4], (N_ATTN_LAYERS, D_MODEL, ATTN_IN_DIM), D_MODEL ** -0.5)
    attn_q_gain = 1.0 + nrm(ks[5], (N_ATTN_LAYERS, ATTN_HEAD_DIM), 0.05)
    attn_k_gain = 1.0 + nrm(ks[6], (N_ATTN_LAYERS, ATTN_HEAD_DIM), 0.05)
    attn_sinks = nrm(ks[7], (N_ATTN_LAYERS, ATTN_HEADS), 1.0)
    attn_w_out = nrm(ks[8], (N_ATTN_LAYERS, ATTN_Q_DIM, D_MODEL), ATTN_Q_DIM ** -0.5)

    gdn_w_in = nrm(ks[9], (N_GDN_LAYERS, D_MODEL, GDN_IN_DIM), D_MODEL ** -0.5)
    gdn_conv = nrm(ks[10], (N_GDN_LAYERS, GDN_CONV, GDN_QKV_DIM), GDN_CONV ** -0.5)
    gdn_a_log = jnp.log(jax.random.uniform(ks[11], (N_GDN_LAYERS, GDN_V_HEADS), f32, 1.0, 16.0))
    dt = jnp.exp(jax.random.uniform(ks[12], (N_GDN_LAYERS, GDN_V_HEADS), f32,
                                    math.log(1e-3), math.log(1e-1)))
    gdn_dt_bias = dt + jnp.log(-jnp.expm1(-dt))
    gdn_norm = 1.0 + nrm(ks[13], (N_GDN_LAYERS, GDN_HEAD_V), 0.05)
    gdn_w_out = nrm(ks[14], (N_GDN_LAYERS, GDN_VAL_DIM, D_MODEL), GDN_VAL_DIM ** -0.5)

    ffn_w_up = nrm(ks[15], (DEPTH, D_MODEL, 2 * D_FF), D_MODEL ** -0.5)
    ffn_conv = nrm(ks[16], (DEPTH, FFN_CONV, D_FF), FFN_CONV ** -0.5)
    ffn_conv_b = nrm(ks[17], (DEPTH, D_FF), 0.02)
    ffn_w_down = nrm(ks[18], (DEPTH, D_FF, D_MODEL), D_FF ** -0.5)
    return {"x": x, "positions": positions, "mixer_norm": mixer_norm, "ffn_norm": ffn_norm,
            "attn_w_in": attn_w_in, "attn_q_gain": attn_q_gain, "attn_k_gain": attn_k_gain,
            "attn_sinks": attn_sinks, "attn_w_out": attn_w_out,
            "gdn_w_in": gdn_w_in, "gdn_conv": gdn_conv, "gdn_a_log": gdn_a_log,
            "gdn_dt_bias": gdn_dt_bias, "gdn_norm": gdn_norm, "gdn_w_out": gdn_w_out,
            "ffn_w_up": ffn_w_up, "ffn_conv": ffn_conv, "ffn_conv_b": ffn_conv_b,
            "ffn_w_down": ffn_w_down}


def reference(x, positions, mixer_norm, ffn_norm, attn_w_in, attn_q_gain, attn_k_gain,
              attn_sinks, attn_w_out, gdn_w_in, gdn_conv, gdn_a_log, gdn_dt_bias, gdn_norm,
              gdn_w_out, ffn_w_up, ffn_conv, ffn_conv_b, ffn_w_down):
    cos, sin = rope_tables(positions)
    h = x
    for i in range(DEPTH):
        j = i // 2
        u = rms_norm(h, mixer_norm[i])
        if i % 2 == 0:
            h = h + swa_sink_attention(u, attn_w_in[j], attn_q_gain[j], attn_k_gain[j],
                                       attn_sinks[j], attn_w_out[j], cos, sin)
        else:
            h = h + gated_deltanet(u, gdn_w_in[j], gdn_conv[j], gdn_a_log[j], gdn_dt_bias[j],
                                   gdn_norm[j], gdn_w_out[j])
        u = rms_norm(h, ffn_norm[i])
        h = h + conv_glu_ffn(u, ffn_w_up[i], ffn_conv[i], ffn_conv_b[i], ffn_w_down[i])
    return h
```

```python
import contextlib
import numpy as np
import ml_dtypes
import concourse.bass as bass
import concourse.mybir as mybir
from concourse.bass_utils import run_bass_kernel_spmd

F32 = mybir.dt.float32
BF16 = mybir.dt.bfloat16
I32 = mybir.dt.int32
ALU = mybir.AluOpType
AF = mybir.ActivationFunctionType
AX = mybir.AxisListType

SEM_CHUNK = 20000
DMA_CHUNK = 1500
EPS = 1e-6
NCORES = 8


class T:
    def __init__(self, h, name):
        self.h = h
        self.name = name
        self.last_w = None
        self.reads = []

    def __getitem__(self, idx):
        return self.h[idx]


class Op:
    __slots__ = ("eng", "fn", "deps", "is_dma", "k", "needs_inc", "chan")

    def __init__(self, eng, fn, deps, is_dma, chan):
        self.eng = eng
        self.fn = fn
        self.deps = deps
        self.is_dma = is_dma
        self.chan = chan
        self.k = None
        self.needs_inc = False


class Prog:
    ENGS = ("pe", "act", "dve", "pool", "sp")

    def __init__(self, nc, same_engine_sync=True):
        self.nc = nc
        self.ops = []
        self.same_engine_sync = same_engine_sync
        self.stack = contextlib.ExitStack()
        self.n_dma_chan = 8
        self._dma_rr = 0
        self.final_waits = []
        self._uid = 0
        self.max_ops = None

    def sbuf(self, name, shape, dtype):
        h = self.stack.enter_context(self.nc.sbuf_tensor(name, list(shape), dtype))
        return T(h, name)

    def psum(self, name, shape, dtype=F32):
        h = self.stack.enter_context(self.nc.psum_tensor(name, list(shape), dtype))
        return T(h, name)

    def op(self, eng, fn, reads=(), writes=(), is_dma=False, chan=None):
        deps = {}
        for t in reads:
            if t.last_w is not None:
                deps[t.last_w] = True
        for t in writes:
            if t.last_w is not None:
                deps.setdefault(t.last_w, False)
            for r in t.reads:
                deps.setdefault(r, False)
        oid = len(self.ops)
        if self.max_ops is not None and oid >= self.max_ops:
            return None
        if is_dma and chan is None:
            chan = "dma_%s%d" % (eng, self._dma_rr % self.n_dma_chan)
            self._dma_rr += 1
        self.ops.append(Op(eng, fn, deps, is_dma, chan if is_dma else eng))
        for t in reads:
            t.reads.append(oid)
        for t in writes:
            t.last_w = oid
            t.reads = []
        return oid

    def dma(self, out_ap, in_ap, reads=(), writes=(), q="sp", chan=None, **kw):
        return self.op(q, lambda e: e.dma_start(out=out_ap, in_=in_ap, **kw),
                       reads, writes, is_dma=True, chan=chan)

    def wait_final(self, oid, eng="sp"):
        if oid is not None:
            self.final_waits.append((eng, oid))

    def _skip_same(self, p, o, raw):
        return ((not p.is_dma) and (not o.is_dma) and p.eng == o.eng
                and (o.eng == "pe" or not self.same_engine_sync or not raw))

    def build(self):
        nc = self.nc
        ops = self.ops
        for eng, oid in self.final_waits:
            ops[oid].needs_inc = True
        for o in ops:
            for d, raw in o.deps.items():
                p = ops[d]
                if self._skip_same(p, o, raw):
                    continue
                p.needs_inc = True
        chan_count = {}
        for o in ops:
            if o.is_dma:
                o.needs_inc = True
            if o.needs_inc:
                c = chan_count.get(o.chan, 0)
                o.k = c
                chan_count[o.chan] = c + 1
        sems = {}
        for ch, n in chan_count.items():
            CH = DMA_CHUNK if ch.startswith("dma") else SEM_CHUNK
            nsem = (n + CH - 1) // CH
            sems[ch] = [self.stack.enter_context(nc.semaphore("s_%s_%d" % (ch, j)))
                        for j in range(nsem)]
        per_eng = {e: [] for e in self.ENGS}
        waited = {e: {} for e in self.ENGS}
        for o in ops:
            need = {}
            for d, raw in o.deps.items():
                p = ops[d]
                if not p.needs_inc or self._skip_same(p, o, raw):
                    continue
                if p.chan not in need or need[p.chan] < p.k:
                    need[p.chan] = p.k
            if o.is_dma and o.k > 0:
                if need.get(o.chan, -1) < o.k - 1:
                    need[o.chan] = o.k - 1
            waits = []
            for ch, k in need.items():
                if waited[o.eng].get(ch, -1) >= k:
                    continue
                waited[o.eng][ch] = k
                waits.append((ch, k))
            per_eng[o.eng].append((waits, o))
        finals = {e: {} for e in self.ENGS}
        for eng, oid in self.final_waits:
            ch = ops[oid].chan
            finals[eng][ch] = max(finals[eng].get(ch, -1), ops[oid].k)

        def semval(ch, k):
            mult = 16 if ch.startswith("dma") else 1
            CH = DMA_CHUNK if mult == 16 else SEM_CHUNK
            return sems[ch][k // CH], (k % CH + 1) * mult

        def emit(engname):
            def body(e):
                for waits, o in per_eng[engname]:
                    for ch, k in waits:
                        s, v = semval(ch, k)
                        e.wait_ge(s, v)
                    ins = o.fn(e)
                    if o.needs_inc:
                        s, _ = semval(o.chan, o.k)
                        ins.then_inc(s, 16 if o.is_dma else 1)
                for ch, k in finals[engname].items():
                    s, v = semval(ch, k)
                    e.wait_ge(s, v)
            return body

        allsems = [x for v in sems.values() for x in v]
        for x in allsems:
            nc.gpsimd.sem_clear(x)
        nc.all_engine_barrier()
        with nc.Block() as block:
            if per_eng["sp"] or finals["sp"]:
                block.sync(emit("sp"))
            if per_eng["pe"]:
                block.tensor(emit("pe"))
            if per_eng["act"]:
                block.scalar(emit("act"))
            if per_eng["dve"]:
                block.vector(emit("dve"))
            if per_eng["pool"]:
                block.gpsimd(emit("pool"))
        nc.all_engine_barrier()
        for x in allsems:
            nc.gpsimd.sem_clear(x)
        nc.all_engine_barrier()
        self.stack.close()
        return nc


class Rot:
    def __init__(self, tiles):
        self.tiles = tiles
        self.i = 0

    def next(self):
        t = self.tiles[self.i % len(self.tiles)]
        self.i += 1
        return t


class WStream:
    def __init__(self, P, stage_elems=3072, nbuf=2):
        self.P = P
        self.stage = Rot([P.sbuf("wst%d" % i, [128, stage_elems], F32) for i in range(nbuf)])
        self.wbf = Rot([P.sbuf("wbf%d" % i, [128, stage_elems], BF16) for i in range(nbuf)])
        self.n = 0

    def panel(self, w_ap, kc, c0, ncols):
        P = self.P
        st = self.stage.next()
        wb = self.wbf.next()
        src = w_ap.rearrange("(c p) n -> p c n", p=128)[:, :, c0:c0 + ncols]
        dst = st[:, 0:kc * ncols].rearrange("p (c n) -> p c n", c=kc)
        P.dma(dst, src, writes=[st], q="sp")
        P.op("pool", lambda e: e.tensor_copy(out=wb[:, 0:kc * ncols], in_=st[:, 0:kc * ncols]), [st], [wb])
        view = wb[:, 0:kc * ncols].rearrange("p (c n) -> p c n", c=kc)
        return wb, view


def mm_group(P, ps, wb, wv, mi, xs, c0, n):
    nc = P.nc
    kc = len(xs)

    def mm(e):
        ins = None
        with nc.allow_low_precision("bf16 matmul"):
            for c in range(kc):
                ins = e.matmul(ps[:, 0:n], lhsT=wv[:, c, mi * 128:(mi + 1) * 128],
                               rhs=xs[c][:, c0:c0 + n], start=(c == 0), stop=(c == kc - 1))
        return ins
    P.op("pe", mm, [wb] + list(xs), [ps])


def linear_fm(P, ws, pspool, w_ap, n_out, ncols, xs, blocks, evac, col0=0):
    kc = len(xs)
    for p0 in range(0, n_out, ncols):
        wb, wv = ws.panel(w_ap, kc, col0 + p0, ncols)
        for mi in range(ncols // 128):
            m = p0 // 128 + mi
            for bi, (c0, n) in enumerate(blocks):
                ps = pspool.next()
                mm_group(P, ps, wb, wv, mi, xs, c0, n)
                evac(m, bi, ps, c0, n)


def rmsnorm_fm(P, pspool, hs, us, gain_t, ones_t, eps_t, sqpool, rstd_t, blocks, D):
    nch = len(hs)
    for (c0, n) in blocks:
        _rms_block(P, pspool.next(), hs, us, gain_t, ones_t, eps_t, sqpool, rstd_t, c0, n, D, nch)


def _rms_block(P, ps, hs, us, gain_t, ones_t, eps_t, sqpool, rstd_t, c0, n, D, nch):
    for c in range(nch):
        sq = sqpool.next()
        P.op("act", lambda e, sq=sq, c=c: e.activation(out=sq[:, 0:n], in_=hs[c][:, c0:c0 + n], func=AF.Square),
             [hs[c]], [sq])
        P.op("pe", lambda e, sq=sq, c=c: e.matmul(ps[:, 0:n], lhsT=ones_t[:], rhs=sq[:, 0:n],
                                                  start=(c == 0), stop=(c == nch - 1)),
             [sq, ones_t], [ps])
    P.op("act", lambda e: e.activation(out=rstd_t[:, 0:n], in_=ps[:, 0:n], func=AF.Sqrt,
                                       bias=eps_t[:, 0:1], scale=1.0 / D), [ps, eps_t], [rstd_t])
    P.op("dve", lambda e: e.reciprocal(out=rstd_t[:, 0:n], in_=rstd_t[:, 0:n]), [rstd_t], [rstd_t])
    for c in range(nch):
        P.op("dve", lambda e, c=c: e.scalar_tensor_tensor(out=us[c][:, c0:c0 + n], in0=hs[c][:, c0:c0 + n],
                                                          scalar=gain_t[:, c:c + 1], in1=rstd_t[:, 0:n],
                                                          op0=ALU.mult, op1=ALU.mult),
             [hs[c], gain_t, rstd_t], [us[c]])


NT = 1024
NPIECE = 2
HALO = 2
TTOK = NT + HALO
DFF = 3072


def build_F(Dm, dbg=0):
    nc = bass.Bass("TRN2", target_bir_lowering=False)
    kcm = Dm // 128
    hT = nc.dram_tensor("hT", [NPIECE, 1024, TTOK], F32, kind="ExternalInput").ap()
    moT = nc.dram_tensor("moT", [NPIECE, Dm, TTOK], BF16, kind="ExternalInput").ap()
    w_out = nc.dram_tensor("w_out", [Dm, 1024], F32, kind="ExternalInput").ap()
    w_up = nc.dram_tensor("w_up", [1024, 2 * DFF], F32, kind="ExternalInput").ap()
    w_down = nc.dram_tensor("w_down", [DFF, 1024], F32, kind="ExternalInput").ap()
    gain = nc.dram_tensor("gain", [128, 8], F32, kind="ExternalInput").ap()
    cw = nc.dram_tensor("cw", [128, 24, 4], F32, kind="ExternalInput").ap()
    out = nc.dram_tensor("out", [NPIECE, 1024, NT], F32, kind="ExternalOutput").ap()

    P = Prog(nc)
    hs = [P.sbuf("h%d" % c, [128, TTOK], F32) for c in range(8)]
    acts = [P.sbuf("a%d" % c, [128, TTOK], BF16) for c in range(24)]
    us = [P.sbuf("u%d" % c, [128, TTOK], BF16) for c in range(8)]
    gain_sb = P.sbuf("gain_sb", [128, 8], F32)
    cw_sb = P.sbuf("cw_sb", [128, 24, 4], F32)
    ones = P.sbuf("ones", [128, 128], F32)
    eps_t = P.sbuf("eps", [128, 1], F32)
    rstd = P.sbuf("rstd", [128, 512], F32)
    sqpool = Rot([P.sbuf("sq%d" % i, [128, 512], F32) for i in range(2)])
    Gpool = Rot([P.sbuf("G%d" % i, [128, TTOK], F32) for i in range(2)])
    Cpool = Rot([P.sbuf("C%d" % i, [128, NT], F32) for i in range(2)])
    Spool = Rot([P.sbuf("S%d" % i, [128, NT], F32) for i in range(3)])
    hout = Rot([P.sbuf("ho%d" % i, [128, 512], F32) for i in range(2)])
    pspool = Rot([P.psum("ps%d" % i, [128, 512]) for i in range(8)])
    ws = WStream(P, stage_elems=3072)

    P.dma(gain_sb[:], gain, writes=[gain_sb], q="act")
    P.dma(cw_sb[:], cw, writes=[cw_sb], q="act")
    P.op("pool", lambda e: e.memset(ones[:], 1.0), [], [ones])
    P.op("pool", lambda e: e.memset(eps_t[:], EPS), [], [eps_t])

    main_blocks = [(HALO, 512), (HALO + 512, 512)]
    all_blocks = [(0, HALO)] + main_blocks
    outs = []
    for pc in range(NPIECE):
        for c in range(8):
            P.dma(hs[c][:], hT[pc, c * 128:(c + 1) * 128, :], writes=[hs[c]], q="act")
        for c in range(kcm):
            P.dma(acts[c][:], moT[pc, c * 128:(c + 1) * 128, :], writes=[acts[c]], q="act")

        def ev1(m, bi, ps, c0, n):
            P.op("dve", lambda e: e.tensor_tensor(out=hs[m][:, c0:c0 + n], in0=ps[:, 0:n], in1=hs[m][:, c0:c0 + n],
                                                  op=ALU.add), [ps, hs[m]], [hs[m]])
        linear_fm(P, ws, pspool, w_out, 1024, 128 if kcm > 8 else 256, acts[0:kcm], all_blocks, ev1)
        if dbg == 1:
            for m in range(8):
                outs.append(P.dma(out[pc, m * 128:(m + 1) * 128, :], hs[m][:, HALO:TTOK], reads=[hs[m]], q="sp"))
            continue
        rmsnorm_fm(P, pspool, hs, us, gain_sb, ones, eps_t, sqpool, rstd, all_blocks, 1024.0)
        if dbg == 2:
            for m in range(8):
                for (c0, n) in main_blocks:
                    ho = hout.next()
                    P.op("dve", lambda e, ho=ho, m=m, c0=c0, n=n: e.tensor_copy(out=ho[:, 0:n], in_=us[m][:, c0:c0 + n]), [us[m]], [ho])
                    outs.append(P.dma(out[pc, m * 128:(m + 1) * 128, c0 - HALO:c0 - HALO + n], ho[:, 0:n], reads=[ho], q="sp"))
            continue

        for j0 in range(0, 24, 2):
            Ss = {}
            for half in (0, 1):
                p0 = half * DFF + j0 * 128
                wb, wv = ws.panel(w_up, 8, p0, 256)
                for mi in range(2):
                    j = j0 + mi
                    if half == 0:
                        G = Gpool.next()
                        for (c0, n) in all_blocks:
                            ps = pspool.next()
                            mm_group(P, ps, wb, wv, mi, us, c0, n)
                            P.op("act", lambda e, ps=ps, c0=c0, n=n, G=G: e.copy(out=G[:, c0:c0 + n], in_=ps[:, 0:n]),
                                 [ps], [G])
                        C = Cpool.next()
                        S = Spool.next()
                        Ss[j] = S
                        P.op("act", lambda e, C=C, G=G, j=j: e.activation(out=C[:], in_=G[:, 2:TTOK], func=AF.Identity,
                                                                           bias=cw_sb[:, j, 3:4], scale=cw_sb[:, j, 2:3]),
                             [G, cw_sb], [C])
                        P.op("dve", lambda e, C=C, G=G, j=j: e.scalar_tensor_tensor(
                            out=C[:], in0=G[:, 1:TTOK - 1], scalar=cw_sb[:, j, 1:2], in1=C[:], op0=ALU.mult, op1=ALU.add),
                            [G, cw_sb, C], [C])
                        P.op("dve", lambda e, C=C, G=G, j=j: e.scalar_tensor_tensor(
                            out=C[:], in0=G[:, 0:TTOK - 2], scalar=cw_sb[:, j, 0:1], in1=C[:], op0=ALU.mult, op1=ALU.add),
                            [G, cw_sb, C], [C])
                        P.op("act", lambda e, C=C, S=S: e.activation(out=S[:], in_=C[:], func=AF.Silu), [C], [S])
                    else:
                        S = Ss[j]
                        for (c0, n) in main_blocks:
                            ps = pspool.next()
                            mm_group(P, ps, wb, wv, mi, us, c0, n)
                            P.op("dve", lambda e, ps=ps, c0=c0, n=n, S=S, j=j: e.tensor_tensor(
                                out=acts[j][:, c0:c0 + n], in0=ps[:, 0:n], in1=S[:, c0 - HALO:c0 - HALO + n], op=ALU.mult),
                                [ps, S], [acts[j]])

        def ev4(m, bi, ps, c0, n):
            ho = hout.next()
            P.op("dve", lambda e: e.tensor_tensor(out=ho[:, 0:n], in0=ps[:, 0:n], in1=hs[m][:, c0:c0 + n], op=ALU.add),
                 [ps, hs[m]], [ho])
            outs.append(P.dma(out[pc, m * 128:(m + 1) * 128, c0 - HALO:c0 - HALO + n], ho[:, 0:n], reads=[ho], q="sp"))
        linear_fm(P, ws, pspool, w_down, 1024, 128, acts, main_blocks, ev4)
    for o in outs:
        P.wait_final(o)
    P.build()
    return nc


_NC_CACHE = {}


def _get_nc(key, builder, *args):
    return builder(*args)


def _launch(nc, in_maps):
    res = run_bass_kernel_spmd(nc, in_maps, core_ids=list(range(NCORES)))
    return res.results


def _with_halo(xT, b, t0, n, halo):
    D = xT.shape[1]
    o = np.zeros((D, halo + n), dtype=xT.dtype)
    lo = max(t0 - halo, 0)
    o[:, halo - (t0 - lo):] = xT[b, :, lo:t0 + n]
    return o


def run_F(hT, moT, w_out, w_up, conv_w, conv_b, w_down, gain, dbg=0):
    B, _, S = hT.shape
    Dm = moT.shape[1]
    nc = build_F(Dm, dbg)
    gain_l = np.ascontiguousarray(gain.reshape(8, 128).T)
    cw = np.concatenate([conv_w, conv_b[None, :]], axis=0)
    cw_l = np.ascontiguousarray(cw.reshape(4, 24, 128).transpose(2, 1, 0))
    per = S * B // NCORES
    in_maps = []
    for c in range(NCORES):
        b = (c * per) // S
        t0c = (c * per) % S
        hp = np.stack([_with_halo(hT, b, t0c + p * NT, NT, HALO) for p in range(NPIECE)])
        mp = np.stack([_with_halo(moT, b, t0c + p * NT, NT, HALO) for p in range(NPIECE)])
        in_maps.append({"hT": hp, "moT": mp, "w_out": w_out, "w_up": w_up, "w_down": w_down,
                        "gain": gain_l, "cw": cw_l})
    res = _launch(nc, in_maps)
    out = np.empty_like(hT)
    for c in range(NCORES):
        b = (c * per) // S
        t0c = (c * per) % S
        for p in range(NPIECE):
            out[b, :, t0c + p * NT:t0c + (p + 1) * NT] = res[c]["out"][p]
    return out


import math
SEQ = 8192
TB = 512
C1_2PI = 6.28125
C2_2PI = 2 * math.pi - C1_2PI


def rope_table(P, out_t, pos_f, invf, shift, tt, ki, kf, n):
    P.op("dve", lambda e: e.tensor_scalar(out=out_t[:, 0:n], in0=pos_f[:, 0:n], scalar1=invf[:, 0:1], scalar2=shift,
                                          op0=ALU.mult, op1=ALU.add), [pos_f, invf], [out_t])
    P.op("dve", lambda e: e.tensor_scalar(out=tt[:, 0:n], in0=out_t[:, 0:n], scalar1=1.0 / (2 * math.pi), scalar2=None,
                                          op0=ALU.mult), [out_t], [tt])
    P.op("dve", lambda e: e.tensor_copy(out=ki[:, 0:n], in_=tt[:, 0:n]), [tt], [ki])
    P.op("dve", lambda e: e.tensor_copy(out=kf[:, 0:n], in_=ki[:, 0:n]), [ki], [kf])
    P.op("dve", lambda e: e.scalar_tensor_tensor(out=out_t[:, 0:n], in0=kf[:, 0:n], scalar=-C1_2PI, in1=out_t[:, 0:n],
                                                 op0=ALU.mult, op1=ALU.add), [kf, out_t], [out_t])
    P.op("dve", lambda e: e.scalar_tensor_tensor(out=out_t[:, 0:n], in0=kf[:, 0:n], scalar=-C2_2PI, in1=out_t[:, 0:n],
                                                 op0=ALU.mult, op1=ALU.add), [kf, out_t], [out_t])
    P.op("dve", lambda e: e.tensor_scalar(out=tt[:, 0:n], in0=out_t[:, 0:n], scalar1=math.pi, scalar2=2 * math.pi,
                                          op0=ALU.is_gt, op1=ALU.mult), [out_t], [tt])
    P.op("dve", lambda e: e.tensor_tensor(out=out_t[:, 0:n], in0=out_t[:, 0:n], in1=tt[:, 0:n], op=ALU.subtract),
         [out_t, tt], [out_t])
    P.op("act", lambda e: e.activation(out=out_t[:, 0:n], in_=out_t[:, 0:n], func=AF.Sin), [out_t], [out_t])


def build_A(nsteps=SEQ // TB, dbg=0):
    nc = bass.Bass("TRN2", target_bir_lowering=False)
    hT = nc.dram_tensor("hT", [1024, SEQ], F32, kind="ExternalInput").ap()
    wqkv = nc.dram_tensor("wqkv", [1024, 448], F32, kind="ExternalInput").ap()
    gain = nc.dram_tensor("gain", [128, 8], F32, kind="ExternalInput").ap()
    qkg = nc.dram_tensor("qkg", [128, 2], F32, kind="ExternalInput").ap()
    pos = nc.dram_tensor("pos", [1, SEQ], I32, kind="ExternalInput").ap()
    invf = nc.dram_tensor("invf", [128, 1], F32, kind="ExternalInput").ap()
    rotT = nc.dram_tensor("rotT", [128, 128], F32, kind="ExternalInput").ap()
    onesbd = nc.dram_tensor("onesbd", [128, 128], F32, kind="ExternalInput").ap()
    sinks = nc.dram_tensor("sinks", [64, 4], F32, kind="ExternalInput").ap()
    masks = nc.dram_tensor("masks", [128, 2, 128], F32, kind="ExternalInput").ap()
    oT = nc.dram_tensor("oT", [256, SEQ], BF16, kind="ExternalOutput").ap()

    P = Prog(nc)
    hs = [P.sbuf("h%d" % c, [128, TB], F32) for c in range(8)]
    us = [P.sbuf("u%d" % c, [128, TB], BF16) for c in range(8)]
    wst = P.sbuf("wst", [128, 8, 448], F32)
    wbf = P.sbuf("wbf", [128, 8, 448], BF16)
    gain_sb = P.sbuf("gain_sb", [128, 8], F32)
    qkg_sb = P.sbuf("qkg_sb", [128, 2], F32)
    invf_sb = P.sbuf("invf_sb", [128, 1], F32)
    rot_sb = P.sbuf("rot_sb", [128, 128], F32)
    obd_sb = P.sbuf("obd_sb", [128, 128], F32)
    sink_sb = P.sbuf("sink_sb", [64, 4], F32)
    esink = P.sbuf("esink", [64, 4], F32)
    mask_sb = P.sbuf("mask_sb", [128, 2, 128], F32)
    mask_bf = P.sbuf("mask_bf", [128, 2, 128], BF16)
    ones = P.sbuf("ones", [128, 128], F32)
    ones_bf = P.sbuf("ones_bf", [128, 64], BF16)
    eps_t = P.sbuf("eps", [128, 1], F32)
    rstd = P.sbuf("rstd", [128, TB], F32)
    sqpool = Rot([P.sbuf("sq%d" % i, [128, TB], F32) for i in range(2)])
    posi = P.sbuf("posi", [128, TB], I32)
    posf = P.sbuf("posf", [128, TB], F32)
    cosF = P.sbuf("cosF", [128, TB], F32)
    sinF = P.sbuf("sinF", [128, TB], F32)
    tt = P.sbuf("tt", [128, TB], F32)
    ki = P.sbuf("ki", [128, TB], I32)
    kf = P.sbuf("kf", [128, TB], F32)
    qf = Rot([P.sbuf("qf%d" % i, [128, TB], F32) for i in range(2)])
    qn = Rot([P.sbuf("qn%d" % i, [128, TB], F32) for i in range(2)])
    t1 = Rot([P.sbuf("t1%d" % i, [128, TB], F32) for i in range(2)])
    t2 = Rot([P.sbuf("t2%d" % i, [128, TB], F32) for i in range(2)])
    qT = [P.sbuf("qT%d" % i, [128, TB], BF16) for i in range(2)]
    kTs = [P.sbuf("kT%d" % i, [128, 128 + TB], BF16) for i in range(2)]
    v_sb = P.sbuf("v_sb", [128, 5, 64], BF16)
    Ef = Rot([P.sbuf("Ef%d" % i, [128, 4, 128], BF16) for i in range(4)])
    Eb = Rot([P.sbuf("Eb%d" % i, [128, 4, 128], BF16) for i in range(4)])
    den = Rot([P.sbuf("den%d" % i, [64, 4, 128], F32) for i in range(2)])
    ostep = Rot([P.sbuf("os%d" % i, [64, 4, TB], BF16) for i in range(2)])
    pspool = Rot([P.psum("ps%d" % i, [128, 512]) for i in range(8)])

    for dst, src in ((wst, wqkv.rearrange("(c p) n -> p c n", p=128)), (gain_sb, gain), (qkg_sb, qkg), (invf_sb, invf),
                     (rot_sb, rotT), (obd_sb, onesbd), (sink_sb, sinks), (mask_sb, masks)):
        P.dma(dst[:], src, writes=[dst], q="act")
    P.op("pool", lambda e: e.tensor_copy(out=wbf[:], in_=wst[:]), [wst], [wbf])
    P.op("pool", lambda e: e.tensor_copy(out=mask_bf[:], in_=mask_sb[:]), [mask_sb], [mask_bf])
    P.op("pool", lambda e: e.memset(ones[:], 1.0), [], [ones])
    P.op("pool", lambda e: e.memset(ones_bf[:], 1.0), [], [ones_bf])
    for kT_ in kTs:
        P.op("pool", lambda e, kT_=kT_: e.memset(kT_[:], 0.0), [], [kT_])
    P.op("pool", lambda e: e.memset(eps_t[:], EPS), [], [eps_t])
    P.op("act", lambda e: e.activation(out=esink[:], in_=sink_sb[:], func=AF.Exp), [sink_sb], [esink])

    outs = []
    for s in range(nsteps):
        tok0 = s * TB
        for c in range(8):
            P.dma(hs[c][:], hT[c * 128:(c + 1) * 128, tok0:tok0 + TB], writes=[hs[c]], q="sp")
        P.dma(posi[:], pos[:, tok0:tok0 + TB].partition_broadcast(128), writes=[posi], q="sp")
        rmsnorm_fm(P, pspool, hs, us, gain_sb, ones, eps_t, sqpool, rstd, [(0, TB)], 1024.0)
        P.op("dve", lambda e: e.tensor_copy(out=posf[:], in_=posi[:]), [posi], [posf])
        rope_table(P, sinF, posf, invf_sb, 0.0, tt, ki, kf, TB)
        rope_table(P, cosF, posf, invf_sb, math.pi / 2, tt, ki, kf, TB)
        if s > 0:
            for kT_ in kTs:
                P.op("pool", lambda e, kT_=kT_: e.tensor_copy(out=kT_[:, 0:128], in_=kT_[:, TB:TB + 128]), [kT_], [kT_])
            P.op("pool", lambda e: e.tensor_copy(out=v_sb[:, 0, :], in_=v_sb[:, 4, :]), [v_sb], [v_sb])
        for m in range(3):
            ps = pspool.next()

            def mmq(e, ps=ps, m=m):
                ins = None
                with nc.allow_low_precision("bf16 matmul"):
                    for c in range(8):
                        ins = e.matmul(ps[:, 0:TB], lhsT=wbf[:, c, m * 128:(m + 1) * 128], rhs=us[c][:, 0:TB],
                                       start=(c == 0), stop=(c == 7))
                return ins
            P.op("pe", mmq, [wbf] + us, [ps])
            f = qf.next()
            P.op("act", lambda e, f=f, ps=ps: e.copy(out=f[:], in_=ps[:, 0:TB]), [ps], [f])
            sq = sqpool.next()
            P.op("act", lambda e, f=f, sq=sq: e.activation(out=sq[:], in_=f[:], func=AF.Square), [f], [sq])
            ps2 = pspool.next()
            P.op("pe", lambda e, ps2=ps2, sq=sq: e.matmul(ps2[:, 0:TB], lhsT=obd_sb[:], rhs=sq[:], start=True, stop=True),
                 [obd_sb, sq], [ps2])
            P.op("act", lambda e, ps2=ps2: e.activation(out=rstd[:], in_=ps2[:, 0:TB], func=AF.Sqrt, bias=eps_t[:, 0:1],
                                                        scale=1.0 / 64), [ps2, eps_t], [rstd])
            P.op("dve", lambda e: e.reciprocal(out=rstd[:], in_=rstd[:]), [rstd], [rstd])
            n_ = qn.next()
            gcol = 0 if m < 2 else 1
            P.op("dve", lambda e, n_=n_, f=f, gcol=gcol: e.scalar_tensor_tensor(
                out=n_[:], in0=f[:], scalar=qkg_sb[:, gcol:gcol + 1], in1=rstd[:], op0=ALU.mult, op1=ALU.mult),
                [f, qkg_sb, rstd], [n_])
            ps3 = pspool.next()
            P.op("pe", lambda e, ps3=ps3, n_=n_: e.matmul(ps3[:, 0:TB], lhsT=rot_sb[:], rhs=n_[:], start=True, stop=True),
                 [rot_sb, n_], [ps3])
            a1 = t1.next()
            a2 = t2.next()
            P.op("dve", lambda e, a1=a1, n_=n_: e.tensor_tensor(out=a1[:], in0=n_[:], in1=cosF[:], op=ALU.mult),
                 [n_, cosF], [a1])
            P.op("dve", lambda e, a2=a2, ps3=ps3: e.tensor_tensor(out=a2[:], in0=ps3[:, 0:TB], in1=sinF[:], op=ALU.mult),
                 [ps3, sinF], [a2])
            if m < 2:
                P.op("pool", lambda e, a1=a1, a2=a2, m=m: e.tensor_tensor(out=qT[m][:, 0:TB], in0=a1[:], in1=a2[:], op=ALU.add),
                     [a1, a2], [qT[m]])
            else:
                for hh in range(2):
                    P.op("pool", lambda e, a1=a1, a2=a2, hh=hh: e.tensor_tensor(
                        out=kTs[hh][hh * 64:hh * 64 + 64, 128:128 + TB], in0=a1[hh * 64:hh * 64 + 64, :],
                        in1=a2[hh * 64:hh * 64 + 64, :], op=ALU.add), [a1, a2], [kTs[hh]])
        for tb in range(4):
            ps = pspool.next()

            def mmv(e, ps=ps, tb=tb):
                ins = None
                with nc.allow_low_precision("bf16 matmul"):
                    for c in range(8):
                        ins = e.matmul(ps[:, 0:64], lhsT=us[c][:, tb * 128:(tb + 1) * 128], rhs=wbf[:, c, 384:448],
                                       start=(c == 0), stop=(c == 7))
                return ins
            P.op("pe", mmv, [wbf] + us, [ps])
            P.op("act", lambda e, ps=ps, tb=tb: e.copy(out=v_sb[:, 1 + tb, :], in_=ps[:, 0:64]), [ps], [v_sb])
        osb = ostep.next()
        if dbg == 1:
            for j in range(4):
                P.op("dve", lambda e, j=j, osb=osb: e.tensor_copy(out=osb[:, j, :], in_=qT[j // 2][(j % 2) * 64:(j % 2) * 64 + 64, :]),
                     [qT[0], qT[1]], [osb])
            outs.append(P.dma(oT.rearrange("(j d) t -> d j t", d=64)[:, :, tok0:tok0 + TB], osb[:], reads=[osb], q="sp"))
            continue
        for qb in range(4):
            gblk = s * 4 + qb
            parts = [(1, 0, 128 + qb * 128, 1 + qb)]
            if gblk > 0:
                parts.append((0, 1, qb * 128, qb))
            Es = []
            for (is_cur, mi, kc0, vb) in parts:
                ps = pspool.next()

                def mms(e, ps=ps, kc0=kc0, qb=qb):
                    ins = None
                    with nc.allow_low_precision("bf16 matmul"):
                        for j in range(4):
                            ins = e.matmul(ps[:, j * 128:(j + 1) * 128], lhsT=kTs[j % 2][:, kc0:kc0 + 128],
                                           rhs=qT[j // 2][:, qb * 128:(qb + 1) * 128], start=True, stop=True)
                    return ins
                P.op("pe", mms, [kTs[0], kTs[1], qT[0], qT[1]], [ps])
                ef = Ef.next()
                P.op("act", lambda e, ef=ef, ps=ps: e.activation(out=ef[:].rearrange("p j q -> p (j q)"), in_=ps[:, 0:512],
                                                                  func=AF.Exp, scale=0.125), [ps], [ef])
                eb = Eb.next()
                P.op("dve", lambda e, ef=ef, eb=eb, mi=mi: e.tensor_tensor(
                    out=eb[:], in0=ef[:], in1=mask_bf[:, mi:mi + 1, :].to_broadcast([128, 4, 128]), op=ALU.mult),
                    [ef, mask_bf], [eb])
                Es.append((eb, vb))
            if dbg == 2:
                eb0 = Es[0][0]
                P.op("dve", lambda e, eb0=eb0, osb=osb, qb=qb: e.tensor_copy(out=osb[:, :, qb * 128:(qb + 1) * 128], in_=eb0[0:64, :, :]),
                     [eb0], [osb])
                continue
            pso = pspool.next()
            psd = pspool.next()

            def mmo(e, pso=pso, psd=psd, Es=Es):
                ins = None
                with nc.allow_low_precision("bf16 matmul"):
                    for i, (eb, vb) in enumerate(Es):
                        e.matmul(pso[0:64, 0:512], lhsT=v_sb[:, vb, :], rhs=eb[:].rearrange("p j q -> p (j q)"),
                                 start=(i == 0), stop=(i == len(Es) - 1))
                    for i, (eb, vb) in enumerate(Es):
                        ins = e.matmul(psd[0:64, 0:512], lhsT=ones_bf[:, 0:64], rhs=eb[:].rearrange("p j q -> p (j q)"),
                                       start=(i == 0), stop=(i == len(Es) - 1))
                return ins
            P.op("pe", mmo, [v_sb, ones_bf] + [e_[0] for e_ in Es], [pso, psd])
            dn = den.next()
            P.op("dve", lambda e, dn=dn, psd=psd: e.tensor_tensor(
                out=dn[:], in0=psd[0:64, 0:512].rearrange("p (j q) -> p j q", j=4),
                in1=esink[:, :].unsqueeze(2).to_broadcast([64, 4, 128]), op=ALU.add), [psd, esink], [dn])
            P.op("dve", lambda e, dn=dn: e.reciprocal(out=dn[:], in_=dn[:]), [dn], [dn])
            P.op("dve", lambda e, dn=dn, pso=pso, osb=osb, qb=qb: e.tensor_tensor(
                out=osb[:, :, qb * 128:(qb + 1) * 128], in0=pso[0:64, 0:512].rearrange("p (j q) -> p j q", j=4),
                in1=dn[:], op=ALU.mult), [pso, dn], [osb])
        outs.append(P.dma(oT.rearrange("(j d) t -> d j t", d=64)[:, :, tok0:tok0 + TB], osb[:], reads=[osb], q="sp"))
    for o in outs:
        P.wait_final(o)
    P.build()
    return nc


def attn_consts():
    invf = np.zeros((128, 1), np.float32)
    inv_freq = (500000.0 ** (-np.arange(0, 16, 2, dtype=np.float32) / 16)).astype(np.float32)
    rotT = np.zeros((128, 128), np.float32)
    for p in range(128):
        d = p % 64
        if d < 16:
            invf[p, 0] = inv_freq[d % 8]
        if d < 8:
            rotT[p + 8, p] = -1.0
        elif d < 16:
            rotT[p - 8, p] = 1.0
    onesbd = np.zeros((128, 128), np.float32)
    onesbd[:64, :64] = 1.0
    onesbd[64:, 64:] = 1.0
    j = np.arange(128)[:, None]
    i = np.arange(128)[None, :]
    masks = np.stack([(j <= i), (j > i)], axis=1).astype(np.float32)
    return invf, rotT, onesbd, masks


def run_A(hT, positions, w_in, q_gain, k_gain, sinks, gain, nsteps=SEQ // TB, dbg=0):
    B, _, S = hT.shape
    nc = build_A(nsteps, dbg)
    invf, rotT, onesbd, masks = attn_consts()
    gain_l = np.ascontiguousarray(gain.reshape(8, 128).T)
    qkg = np.stack([np.tile(q_gain, 2), np.tile(k_gain, 2)], axis=1).astype(np.float32)
    in_maps = []
    for c in range(NCORES):
        b, g = c // 4, c % 4
        wq = w_in[:, g * 256:(g + 1) * 256]
        wk = w_in[:, 1024 + g * 64:1024 + (g + 1) * 64]
        wv = w_in[:, 1280 + g * 64:1280 + (g + 1) * 64]
        in_maps.append({"hT": np.ascontiguousarray(hT[b]), "wqkv": np.ascontiguousarray(np.concatenate([wq, wk, wk, wv], axis=1)),
                        "gain": gain_l, "qkg": qkg, "pos": np.ascontiguousarray(positions[b:b + 1]), "invf": invf, "rotT": rotT,
                        "onesbd": onesbd, "sinks": np.ascontiguousarray(np.tile(sinks[g * 4:(g + 1) * 4][None, :], (64, 1))),
                        "masks": masks})
    res = _launch(nc, in_maps)
    out = np.empty((B, 1024, S), dtype=ml_dtypes.bfloat16)
    for c in range(NCORES):
        b, g = c // 4, c % 4
        out[b, g * 256:(g + 1) * 256, :] = res[c]["oT"]
    return out


def TT(P, eng, out, in0, in1, op, R, W):
    P.op(eng, lambda e: e.tensor_tensor(out=out, in0=in0, in1=in1, op=op), R, W)


def TS(P, eng, out, in0, s1, s2, op0, op1, R, W):
    if op1 is None:
        P.op(eng, lambda e: e.tensor_scalar(out=out, in0=in0, scalar1=s1, scalar2=None, op0=op0), R, W)
    else:
        P.op(eng, lambda e: e.tensor_scalar(out=out, in0=in0, scalar1=s1, scalar2=s2, op0=op0, op1=op1), R, W)


def STT(P, eng, out, in0, scalar, in1, op0, op1, R, W):
    P.op(eng, lambda e: e.scalar_tensor_tensor(out=out, in0=in0, scalar=scalar, in1=in1, op0=op0, op1=op1), R, W)


def ACTV(P, out, in_, func, R, W, **kw):
    P.op("act", lambda e: e.activation(out=out, in_=in_, func=func, **kw), R, W)


def CP(P, eng, out, in_, R, W):
    if eng == "act":
        P.op("act", lambda e: e.copy(out=out, in_=in_), R, W)
    else:
        P.op(eng, lambda e: e.tensor_copy(out=out, in_=in_), R, W)


def MMS(P, mms, R, W, lowp=True):
    nc = P.nc

    def f(e):
        ins = None
        if lowp:
            with nc.allow_low_precision("bf16 matmul"):
                for (o, l, r, st, sp) in mms:
                    ins = e.matmul(o, lhsT=l, rhs=r, start=st, stop=sp)
        else:
            for (o, l, r, st, sp) in mms:
                ins = e.matmul(o, lhsT=l, rhs=r, start=st, stop=sp)
        return ins
    P.op("pe", f, R, W)


def TRANSP(P, out, in_, ident, R, W):
    P.op("pe", lambda e: e.transpose(out, in_, ident), R, W)


GW = 1600
NEGBIG = -30000.0


def build_G(nsteps=SEQ // TB, dbg=0, max_ops=None):
    nc = bass.Bass("TRN2", target_bir_lowering=False)
    hT = nc.dram_tensor("hT", [1024, SEQ], F32, kind="ExternalInput").ap()
    wg = nc.dram_tensor("wg", [1024, GW], F32, kind="ExternalInput").ap()
    gain = nc.dram_tensor("gain", [128, 8], F32, kind="ExternalInput").ap()
    convw = nc.dram_tensor("convw", [128, 8, 4], F32, kind="ExternalInput").ap()
    alog = nc.dram_tensor("alog", [128, 4], F32, kind="ExternalInput").ap()
    dtb = nc.dram_tensor("dtb", [128, 4], F32, kind="ExternalInput").ap()
    ngain = nc.dram_tensor("ngain", [128, 1], F32, kind="ExternalInput").ap()
    cst = nc.dram_tensor("cst", [128, 4, 128], F32, kind="ExternalInput").ap()
    negm = nc.dram_tensor("negm", [128, 4, 128], F32, kind="ExternalInput").ap()
    oT = nc.dram_tensor("oT", [512, SEQ], BF16, kind="ExternalOutput").ap()

    P = Prog(nc)
    P.max_ops = max_ops
    sb = P.sbuf
    hs = [sb("h%d" % c, [128, TB], F32) for c in range(8)]
    us = [sb("u%d" % c, [128, TB], BF16) for c in range(8)]
    wst = sb("wst", [128, 8, 200], F32)
    wbf = sb("wbf", [128, 8, GW], BF16)
    gain_sb = sb("gain_sb", [128, 8], F32)
    convw_sb = sb("convw_sb", [128, 8, 4], F32)
    alog_sb = sb("alog_sb", [128, 4], F32)
    dtb_sb = sb("dtb_sb", [128, 4], F32)
    negA = sb("negA", [128, 4], F32)
    ngain_sb = sb("ngain_sb", [128, 1], F32)
    cst_sb = sb("cst_sb", [128, 4, 128], F32)
    negm_sb = sb("negm_sb", [128, 4, 128], F32)
    ident = cst_sb[:, 0, :]
    LinclT = cst_sb[:, 1, :]
    UsT = cst_sb[:, 2, :]
    strictT = cst_sb[:, 3, :]
    ones = sb("ones", [128, 128], F32)
    eps_t = sb("eps", [128, 1], F32)
    one_c = sb("one_c", [128, 1], F32)
    rstd = sb("rstd", [128, TB], F32)
    sqpool = Rot([sb("sq%d" % i, [128, TB], F32) for i in range(2)])
    X = [sb("X%d" % m, [128, 3 + TB], F32) for m in range(8)]
    Yc = Rot([sb("Yc%d" % i, [128, TB], F32) for i in range(2)])
    qkf = [sb("qkf%d" % m, [128, TB], F32) for m in range(4)]
    qTb = [sb("qTb%d" % m, [128, TB], BF16) for m in range(2)]
    kTb = [sb("kTb%d" % m, [128, TB], BF16) for m in range(2)]
    kTf = [sb("kTf%d" % m, [128, TB], F32) for m in range(2)]
    vTf = [sb("vTf%d" % m, [128, TB], F32) for m in range(4)]
    zs = [sb("zs%d" % m, [128, TB], F32) for m in range(4)]
    Oacc = [sb("Oacc%d" % m, [128, TB], F32) for m in range(4)]
    oout = Rot([sb("oout%d" % i, [128, TB], BF16) for i in range(2)])
    ba = sb("ba", [128, 8], F32)
    cols = sb("cols", [128, 8, 4], F32)
    tmp4 = sb("tmp4", [128, 4], F32)
    rhsb = Rot([sb("rhsb%d" % i, [128, 4, 128], F32) for i in range(2)])
    egcb = sb("egcb", [128, 4, 128], F32)
    betab = sb("betab", [128, 4, 128], F32)
    kks = [sb("kks%d" % i, [128, 128], F32) for i in range(2)]
    DT = [sb("DT%d" % h, [128, 128], F32) for h in range(4)]
    Mt = [sb("Mt%d" % h, [128, 128], F32) for h in range(4)]
    QQ = [Rot([sb("QQ%d_%d" % (h, i), [128, 2, 128], F32) for i in range(2)]) for h in range(4)]
    XX = [Rot([sb("XX%d_%d" % (h, i), [128, 128], F32) for i in range(2)]) for h in range(4)]
    TTb = [sb("TTb%d" % h, [128, 128], F32) for h in range(4)]
    kbg = [sb("kbg%d" % h, [128, 128], F32) for h in range(4)]
    kd = [[sb("kd%d_%d" % (h, c), [128, 128], F32) for c in range(2)] for h in range(4)]
    vb = [sb("vb%d" % h, [128, 128], F32) for h in range(4)]
    uw = [sb("uw%d" % h, [128, 256], F32) for h in range(4)]
    qd = [sb("qd%d" % h, [128, 128], F32) for h in range(4)]
    qkm = [sb("qkm%d" % h, [128, 128], F32) for h in range(4)]
    vnew = [sb("vnew%d" % h, [128, 128], F32) for h in range(4)]
    S = [sb("S%d" % h, [128, 128], F32) for h in range(4)]
    psA = Rot([P.psum("psA%d" % i, [128, 512]) for i in range(2)])
    psB = Rot([P.psum("psB%d" % i, [128, 512]) for i in range(5)])
    psO = P.psum("psO", [128, 512])

    q_ = "act"
    for dst, src in ((gain_sb, gain), (convw_sb, convw), (alog_sb, alog), (dtb_sb, dtb), (ngain_sb, ngain),
                     (cst_sb, cst), (negm_sb, negm)):
        P.dma(dst[:], src, writes=[dst], q=q_)
    wsrc = wg.rearrange("(c p) n -> p c n", p=128)
    for c0 in range(0, GW, 200):
        P.dma(wst[:], wsrc[:, :, c0:c0 + 200], writes=[wst], q="sp")
        CP(P, "pool", wbf[:, :, c0:c0 + 200], wst[:], [wst], [wbf])
    P.op("pool", lambda e: e.memset(ones[:], 1.0), [], [ones])
    P.op("pool", lambda e: e.memset(eps_t[:], EPS), [], [eps_t])
    P.op("pool", lambda e: e.memset(one_c[:], 1.0), [], [one_c])
    for t in X:
        P.op("pool", lambda e, t=t: e.memset(t[:, 0:3], 0.0), [], [t])
    for h in range(4):
        for t in (S[h], vnew[h], kd[h][0], kd[h][1]):
            P.op("pool", lambda e, t=t: e.memset(t[:], 0.0), [], [t])
    ACTV(P, negA[:], alog_sb[:], AF.Exp, [alog_sb], [negA])
    TS(P, "dve", negA[:], negA[:], -1.0, None, ALU.mult, None, [negA], [negA])

    outs = []
    for s in range(nsteps):
        tok0 = s * TB
        for c in range(8):
            P.dma(hs[c][:], hT[c * 128:(c + 1) * 128, tok0:tok0 + TB], writes=[hs[c]], q="sp")
        rmsnorm_fm(P, psA, hs, us, gain_sb, ones, eps_t, sqpool, rstd, [(0, TB)], 1024.0)
        for m in range(12):
            ps = psA.next()
            MMS(P, [(ps[:, 0:TB], wbf[:, c, m * 128:(m + 1) * 128], us[c][:, 0:TB], c == 0, c == 7) for c in range(8)],
                [wbf] + us, [ps])
            if m < 8:
                CP(P, "act", X[m][:, 3:3 + TB], ps[:, 0:TB], [ps], [X[m]])
                y = Yc.next()
                ACTV(P, y[:], X[m][:, 3:3 + TB], AF.Identity, [X[m], convw_sb], [y], scale=convw_sb[:, m, 3:4])
                for tap in range(3):
                    STT(P, "dve", y[:], X[m][:, tap:tap + TB], convw_sb[:, m, tap:tap + 1], y[:], ALU.mult, ALU.add,
                        [X[m], convw_sb, y], [y])
                CP(P, "pool", X[m][:, 0:3], X[m][:, TB:TB + 3], [X[m]], [X[m]])
                dst = qkf[m] if m < 4 else vTf[m - 4]
                ACTV(P, dst[:], y[:], AF.Silu, [y], [dst])
            else:
                ACTV(P, zs[m - 8][:], ps[:, 0:TB], AF.Silu, [ps], [zs[m - 8]])
        for m in range(4):
            sq = sqpool.next()
            ACTV(P, sq[:], qkf[m][:], AF.Square, [qkf[m]], [sq])
            ps = psA.next()
            MMS(P, [(ps[:, 0:TB], ones[:], sq[:], True, True)], [ones, sq], [ps], lowp=False)
            ACTV(P, rstd[:], ps[:, 0:TB], AF.Sqrt, [ps, eps_t], [rstd], bias=eps_t[:, 0:1], scale=1.0)
            P.op("dve", lambda e: e.reciprocal(out=rstd[:], in_=rstd[:]), [rstd], [rstd])
            if m < 2:
                STT(P, "dve", qTb[m][:], qkf[m][:], 128.0 ** -0.5, rstd[:], ALU.mult, ALU.mult, [qkf[m], rstd], [qTb[m]])
            else:
                TT(P, "dve", kTf[m - 2][:], qkf[m][:], rstd[:], ALU.mult, [qkf[m], rstd], [kTf[m - 2]])
                CP(P, "pool", kTb[m - 2][:], kTf[m - 2][:], [kTf[m - 2]], [kTb[m - 2]])

        if dbg:
            for h in range(4):
                P.op("pool", lambda e, h=h: e.memset(Oacc[h][:], 1.0), [], [Oacc[h]])
        for sc in range(TB // 128):
            t0 = sc * 128
            tsl = slice(t0, t0 + 128)
            if dbg == 1:
                continue
            ps = psB.next()
            MMS(P, [(ps[:, 0:64], us[c][:, tsl], wbf[:, c, 1536:1600], c == 0, c == 7) for c in range(8)], [wbf] + us, [ps])
            CP(P, "act", ba[:, 0:4], ps[:, 0:4], [ps], [ba])
            CP(P, "act", ba[:, 4:8], ps[:, 32:36], [ps], [ba])
            ACTV(P, cols[:, 0, :], ba[:, 0:4], AF.Sigmoid, [ba], [cols])
            TT(P, "dve", tmp4[:], ba[:, 4:8], dtb_sb[:], ALU.add, [ba, dtb_sb], [tmp4])
            ACTV(P, tmp4[:], tmp4[:], AF.Exp, [tmp4], [tmp4])
            ACTV(P, tmp4[:], tmp4[:], AF.Ln, [tmp4, one_c], [tmp4], bias=one_c[:, 0:1], scale=1.0)
            TT(P, "dve", cols[:, 1, :], tmp4[:], negA[:], ALU.mult, [tmp4, negA], [cols])
            ps = psB.next()
            MMS(P, [(ps[:, 0:4], LinclT, cols[:, 1, :], True, True), (ps[:, 4:8], UsT, cols[:, 1, :], True, True)],
                [cst_sb, cols], [ps], lowp=False)
            CP(P, "act", cols[:, 2:4, :], ps[:, 0:8].rearrange("p (a h) -> p a h", a=2), [ps], [cols])
            ACTV(P, cols[:, 4:6, :], cols[:, 2:4, :], AF.Exp, [cols], [cols])
            TT(P, "dve", cols[:, 6, :], cols[:, 0, :], cols[:, 4, :], ALU.mult, [cols], [cols])
            TS(P, "dve", cols[:, 7, :], cols[:, 2, :], -1.0, None, ALU.mult, None, [cols], [cols])
            r1 = rhsb.next()
            TT(P, "dve", r1[:], cols[:, 1, :].unsqueeze(2).to_broadcast([128, 4, 128]),
               LinclT.unsqueeze(1).to_broadcast([128, 4, 128]), ALU.mult, [cols, cst_sb], [r1])
            psg1 = psA.next()
            MMS(P, [(psg1[:, 0:512], ones[:], r1[:].rearrange("p h i -> p (h i)"), True, True)], [ones, r1], [psg1], lowp=False)
            ACTV(P, egcb[:].rearrange("p h i -> p (h i)"), psg1[:, 0:512], AF.Exp, [psg1], [egcb])
            psg = psA.next()
            MMS(P, [(psg[:, 0:512], ones[:], r1[:].rearrange("p h i -> p (h i)"), True, False),
                    (psg[:, 0:512], ident, negm_sb[:].rearrange("p h i -> p (h i)"), False, True)],
                [ones, r1, cst_sb, negm_sb], [psg], lowp=False)
            r2 = rhsb.next()
            TT(P, "dve", r2[:], cols[:, 0, :].unsqueeze(2).to_broadcast([128, 4, 128]),
               ident.unsqueeze(1).to_broadcast([128, 4, 128]), ALU.mult, [cols, cst_sb], [r2])
            psb = psA.next()
            MMS(P, [(psb[:, 0:512], ones[:], r2[:].rearrange("p h i -> p (h i)"), True, True)], [ones, r2], [psb], lowp=False)
            CP(P, "act", betab[:].rearrange("p h i -> p (h i)"), psb[:, 0:512], [psb], [betab])
            for h in range(4):
                ACTV(P, DT[h][:], psg[:, h * 128:(h + 1) * 128], AF.Exp, [psg, cols], [DT[h]], bias=cols[:, 7, h:h + 1], scale=1.0)
            if dbg == 2:
                continue
            pskq = []
            for qh in range(2):
                ps = psB.next()
                MMS(P, [(ps[:, 0:128], kTb[qh][:, tsl], kTb[qh][:, tsl], True, True),
                        (ps[:, 128:256], kTb[qh][:, tsl], qTb[qh][:, tsl], True, True)], [kTb[qh], qTb[qh]], [ps])
                TT(P, "dve", kks[qh][:], ps[:, 0:128], strictT, ALU.mult, [ps, cst_sb], [kks[qh]])
                pskq.append(ps)
            for h in range(4):
                qh = h // 2
                TT(P, "dve", qkm[h][:], pskq[qh][:, 128:256], DT[h][:], ALU.mult, [pskq[qh], DT[h]], [qkm[h]])
                TT(P, "pool", Mt[h][:], kks[qh][:], DT[h][:], ALU.mult, [kks[qh], DT[h]], [Mt[h]])
                Q = QQ[h].next()
                STT(P, "dve", Q[:, 0, :], Mt[h][:], -1.0, betab[:, h, :], ALU.mult, ALU.mult, [Mt[h], betab], [Q])
                TT(P, "pool", qd[h][:], qTb[qh][:, tsl], egcb[:, h, :], ALU.mult, [qTb[qh], egcb], [qd[h]])
            for qh in range(2):
                ps = psB.next()
                TRANSP(P, ps[:, 0:128], kTf[qh][:, tsl], ident, [kTf[qh], cst_sb], [ps])
                for h in (2 * qh, 2 * qh + 1):
                    TS(P, "act" if False else "dve", kbg[h][:], ps[:, 0:128], cols[:, 6, h:h + 1], None, ALU.mult, None, [ps, cols], [kbg[h]])
                    for c in range(2):
                        TS(P, "pool" if False else "dve", kd[h][c][c * 64:c * 64 + 64, :], ps[c * 64:c * 64 + 64, 0:128],
                           cols[c * 64:c * 64 + 64, 5, h:h + 1], None, ALU.mult, None, [ps, cols], [kd[h][c]])
            for h in range(4):
                ps = psB.next()
                TRANSP(P, ps[:, 0:128], vTf[h][:, tsl], ident, [vTf[h], cst_sb], [ps])
                ACTV(P, vb[h][:], ps[:, 0:128], AF.Identity, [ps, cols], [vb[h]], scale=cols[:, 0, h:h + 1])
            if dbg == 3:
                continue
            Qc = [QQ[h].tiles[(QQ[h].i - 1) % 2] for h in range(4)]
            Xc = [None] * 4
            for h in range(4):
                ps = psB.next()
                TRANSP(P, ps[:, 0:128], Qc[h][:, 0, :], ident, [Qc[h], cst_sb], [ps])
                CP(P, "act", Qc[h][:, 1, :], ps[:, 0:128], [ps], [Qc[h]])
                x = XX[h].next()
                TT(P, "dve", x[:], Qc[h][:, 0, :], ident, ALU.add, [Qc[h], cst_sb], [x])
                Xc[h] = x
            for lvl in range(1, 6):
                for h in range(4):
                    Qo = Qc[h]
                    Qn = QQ[h].next()
                    ps = psB.next()
                    mm = [(ps[:, 128:256], Qo[:, 0, :], Qo[:, 1, :], True, True)]
                    if lvl < 5:
                        mm.append((ps[:, 0:128], Qo[:, 1, :], Qo[:, 0, :], True, True))
                    MMS(P, mm, [Qo], [ps], lowp=False)
                    if lvl < 5:
                        CP(P, "act", Qn[:].rearrange("p a i -> p (a i)"), ps[:, 0:256], [ps], [Qn])
                    else:
                        CP(P, "act", Qn[:, 1, :], ps[:, 128:256], [ps], [Qn])
                    Qc[h] = Qn
                for h in range(4):
                    xo = Xc[h]
                    ps = psB.next()
                    MMS(P, [(ps[:, 0:128], ident, xo[:], True, False), (ps[:, 0:128], Qc[h][:, 1, :], xo[:], False, True)],
                        [cst_sb, Qc[h], xo], [ps], lowp=False)
                    if lvl < 5:
                        xn = XX[h].next()
                        CP(P, "dve", xn[:], ps[:, 0:128], [ps], [xn])
                        Xc[h] = xn
                    else:
                        CP(P, "dve", TTb[h][:], ps[:, 0:128], [ps], [TTb[h]])
            if dbg == 4:
                continue
            for h in range(4):
                ps = psB.next()
                MMS(P, [(ps[:, 0:128], TTb[h][:], vb[h][:], True, True), (ps[:, 128:256], kbg[h][:], TTb[h][:], True, True)],
                    [TTb[h], vb[h], kbg[h]], [ps], lowp=False)
                CP(P, "act", uw[h][:], ps[:, 0:256], [ps], [uw[h]])
            if dbg in (5, 6, 7):
                continue
            for c in range(2):
                r = slice(c * 64, c * 64 + 64)
                for h in range(4):
                    ps = psB.next()
                    MMS(P, [(ps[:, 0:128], uw[h][:, 128:256], S[h][:], True, True)], [uw[h], S[h]], [ps], lowp=False)
                    TT(P, "dve", vnew[h][r, :], uw[h][r, 0:128], ps[r, 0:128], ALU.subtract, [uw[h], ps], [vnew[h]])
                for h in range(4):
                    oc = psO[:, h * 128 + c * 64:h * 128 + c * 64 + 64]
                    MMS(P, [(oc, S[h][:], qd[h][:, r], True, False), (oc, vnew[h][:], qkm[h][:, r], False, True)],
                        [S[h], qd[h], vnew[h], qkm[h]], [psO], lowp=False)
                    ps = psB.next()
                    MMS(P, [(ps[:, 0:128], kd[h][c][:], vnew[h][:], True, True)], [kd[h][c], vnew[h]], [ps], lowp=False)
                    STT(P, "dve", S[h][:], S[h][:], egcb[:, h, c * 64 + 63:c * 64 + 64], ps[:, 0:128], ALU.mult, ALU.add,
                        [S[h], egcb, ps], [S[h]])
            for h in range(4):
                CP(P, "act", Oacc[h][:, tsl], psO[:, h * 128:(h + 1) * 128], [psO], [Oacc[h]])
        for h in range(4):
            sq = sqpool.next()
            ACTV(P, sq[:], Oacc[h][:], AF.Square, [Oacc[h]], [sq])
            ps = psA.next()
            MMS(P, [(ps[:, 0:TB], ones[:], sq[:], True, True)], [ones, sq], [ps], lowp=False)
            ACTV(P, rstd[:], ps[:, 0:TB], AF.Sqrt, [ps, eps_t], [rstd], bias=eps_t[:, 0:1], scale=1.0 / 128)
            P.op("dve", lambda e: e.reciprocal(out=rstd[:], in_=rstd[:]), [rstd], [rstd])
            STT(P, "dve", Oacc[h][:], Oacc[h][:], ngain_sb[:, 0:1], rstd[:], ALU.mult, ALU.mult, [Oacc[h], ngain_sb, rstd], [Oacc[h]])
            oo = oout.next()
            TT(P, "dve", oo[:], Oacc[h][:], zs[h][:], ALU.mult, [Oacc[h], zs[h]], [oo])
            outs.append(P.dma(oT[h * 128:(h + 1) * 128, tok0:tok0 + TB], oo[:], reads=[oo], q="sp"))
    for o in outs:
        P.wait_final(o)
    P.build()
    return nc


def gdn_consts():
    t = np.arange(128)
    same = (t[:, None] // 64) == (t[None, :] // 64)
    ident = np.eye(128, dtype=np.float32)
    LinclT = ((t[:, None] <= t[None, :]) & same).astype(np.float32)
    UsT = ((t[:, None] > t[None, :]) & same).astype(np.float32)
    strictT = ((t[:, None] < t[None, :]) & same).astype(np.float32)
    cst = np.stack([ident, LinclT, UsT, strictT], axis=1)
    neg = np.where((t[:, None] <= t[None, :]) & same, 0.0, NEGBIG).astype(np.float32)
    negm = np.ascontiguousarray(np.tile(neg[:, None, :], (1, 4, 1)))
    return np.ascontiguousarray(cst), negm


def run_G(hT, w_in, conv_w, a_log, dt_bias, norm_gain, gain, nsteps=SEQ // TB, dbg=0, max_ops=None):
    B, _, S = hT.shape
    nc = build_G(nsteps, dbg, max_ops)
    cst, negm = gdn_consts()
    gain_l = np.ascontiguousarray(gain.reshape(8, 128).T)
    in_maps = []
    for c in range(NCORES):
        b, hg = c // 4, c % 4
        qs = slice(hg * 256, (hg + 1) * 256)
        ks = slice(1024 + hg * 256, 1024 + (hg + 1) * 256)
        vs = slice(2048 + hg * 512, 2048 + (hg + 1) * 512)
        zsl = slice(4096 + hg * 512, 4096 + (hg + 1) * 512)
        bs = slice(6144 + hg * 4, 6144 + (hg + 1) * 4)
        as_ = slice(6160 + hg * 4, 6160 + (hg + 1) * 4)
        z28 = np.zeros((1024, 28), np.float32)
        w = np.concatenate([w_in[:, qs], w_in[:, ks], w_in[:, vs], w_in[:, zsl], w_in[:, bs], z28, w_in[:, as_], z28], axis=1)
        cw = np.concatenate([conv_w[:, qs], conv_w[:, ks], conv_w[:, vs]], axis=1)
        cw_l = np.ascontiguousarray(cw.reshape(4, 8, 128).transpose(2, 1, 0))
        in_maps.append({"hT": np.ascontiguousarray(hT[b]), "wg": np.ascontiguousarray(w), "gain": gain_l, "convw": cw_l,
                        "alog": np.ascontiguousarray(np.tile(a_log[hg * 4:(hg + 1) * 4][None, :], (128, 1))),
                        "dtb": np.ascontiguousarray(np.tile(dt_bias[hg * 4:(hg + 1) * 4][None, :], (128, 1))),
                        "ngain": np.ascontiguousarray(norm_gain.reshape(128, 1)), "cst": cst, "negm": negm})
    res = _launch(nc, in_maps)
    out = np.empty((B, 2048, S), dtype=ml_dtypes.bfloat16)
    for c in range(NCORES):
        b, hg = c // 4, c % 4
        out[b, hg * 512:(hg + 1) * 512, :] = res[c]["oT"]
    return out


def kernel(x, positions, mixer_norm, ffn_norm, attn_w_in, attn_q_gain, attn_k_gain, attn_sinks, attn_w_out,
           gdn_w_in, gdn_conv, gdn_a_log, gdn_dt_bias, gdn_norm, gdn_w_out, ffn_w_up, ffn_conv, ffn_conv_b,
           ffn_w_down):
    A = lambda a: np.asarray(a)
    x = A(x).astype(np.float32, copy=False)
    positions = A(positions).astype(np.int32, copy=False)
    (mixer_norm, ffn_norm, attn_w_in, attn_q_gain, attn_k_gain, attn_sinks, attn_w_out, gdn_w_in, gdn_conv, gdn_a_log,
     gdn_dt_bias, gdn_norm, gdn_w_out, ffn_w_up, ffn_conv, ffn_conv_b, ffn_w_down) = [
        A(a).astype(np.float32, copy=False) for a in (
            mixer_norm, ffn_norm, attn_w_in, attn_q_gain, attn_k_gain, attn_sinks, attn_w_out, gdn_w_in, gdn_conv,
            gdn_a_log, gdn_dt_bias, gdn_norm, gdn_w_out, ffn_w_up, ffn_conv, ffn_conv_b, ffn_w_down)]
    hT = np.ascontiguousarray(x.transpose(0, 2, 1))
    depth = mixer_norm.shape[0]
    for i in range(depth):
        j = i // 2
        if i % 2 == 0:
            mo = run_A(hT, positions, attn_w_in[j], attn_q_gain[j], attn_k_gain[j], attn_sinks[j], mixer_norm[i])
            w_out = attn_w_out[j]
        else:
            mo = run_G(hT, gdn_w_in[j], gdn_conv[j], gdn_a_log[j], gdn_dt_bias[j], gdn_norm[j], mixer_norm[i])
            w_out = gdn_w_out[j]
        hT = run_F(hT, mo, np.ascontiguousarray(w_out), np.ascontiguousarray(ffn_w_up[i]), ffn_conv[i], ffn_conv_b[i],
                   np.ascontiguousarray(ffn_w_down[i]), ffn_norm[i])
    return np.ascontiguousarray(hT.transpose(0, 2, 1)).astype(np.float32, copy=False)
```

```python
import contextlib
import numpy as np
import ml_dtypes
import concourse.bass as bass
import concourse.mybir as mybir
from concourse.bass_utils import run_bass_kernel_spmd

F32 = mybir.dt.float32
BF16 = mybir.dt.bfloat16
I32 = mybir.dt.int32
ALU = mybir.AluOpType
AF = mybir.ActivationFunctionType
AX = mybir.AxisListType

SEM_CHUNK = 20000
DMA_CHUNK = 1500
EPS = 1e-6
NCORES = 8


class T:
    def __init__(self, h, name):
        self.h = h
        self.name = name
        self.last_w = None
        self.reads = []

    def __getitem__(self, idx):
        return self.h[idx]


class Op:
    __slots__ = ("eng", "fn", "deps", "is_dma", "k", "needs_inc", "chan")

    def __init__(self, eng, fn, deps, is_dma, chan):
        self.eng = eng
        self.fn = fn
        self.deps = deps
        self.is_dma = is_dma
        self.chan = chan
        self.k = None
        self.needs_inc = False


class Prog:
    ENGS = ("pe", "act", "dve", "pool", "sp")

    def __init__(self, nc, same_engine_sync=True):
        self.nc = nc
        self.ops = []
        self.same_engine_sync = same_engine_sync
        self.stack = contextlib.ExitStack()
        self.n_dma_chan = 8
        self._dma_rr = 0
        self.final_waits = []
        self._uid = 0
        self.max_ops = None

    def sbuf(self, name, shape, dtype):
        h = self.stack.enter_context(self.nc.sbuf_tensor(name, list(shape), dtype))
        return T(h, name)

    def psum(self, name, shape, dtype=F32):
        h = self.stack.enter_context(self.nc.psum_tensor(name, list(shape), dtype))
        return T(h, name)

    def op(self, eng, fn, reads=(), writes=(), is_dma=False, chan=None):
        deps = {}
        for t in reads:
            if t.last_w is not None:
                deps[t.last_w] = True
        for t in writes:
            if t.last_w is not None:
                deps.setdefault(t.last_w, False)
            for r in t.reads:
                deps.setdefault(r, False)
        oid = len(self.ops)
        if self.max_ops is not None and oid >= self.max_ops:
            return None
        if is_dma and chan is None:
            chan = "dma_%s%d" % (eng, self._dma_rr % self.n_dma_chan)
            self._dma_rr += 1
        self.ops.append(Op(eng, fn, deps, is_dma, chan if is_dma else eng))
        for t in reads:
            t.reads.append(oid)
        for t in writes:
            t.last_w = oid
            t.reads = []
        return oid

    def dma(self, out_ap, in_ap, reads=(), writes=(), q="sp", chan=None, **kw):
        return self.op(q, lambda e: e.dma_start(out=out_ap, in_=in_ap, **kw),
                       reads, writes, is_dma=True, chan=chan)

    def wait_final(self, oid, eng="sp"):
        if oid is not None:
            self.final_waits.append((eng, oid))

    def _skip_same(self, p, o, raw):
        return ((not p.is_dma) and (not o.is_dma) and p.eng == o.eng
                and (o.eng == "pe" or not self.same_engine_sync or not raw))

    def build(self):
        nc = self.nc
        ops = self.ops
        for eng, oid in self.final_waits:
            ops[oid].needs_inc = True
        for o in ops:
            for d, raw in o.deps.items():
                p = ops[d]
                if self._skip_same(p, o, raw):
                    continue
                p.needs_inc = True
        chan_count = {}
        for o in ops:
            if o.is_dma:
                o.needs_inc = True
            if o.needs_inc:
                c = chan_count.get(o.chan, 0)
                o.k = c
                chan_count[o.chan] = c + 1
        sems = {}
        for ch, n in chan_count.items():
            CH = DMA_CHUNK if ch.startswith("dma") else SEM_CHUNK
            nsem = (n + CH - 1) // CH
            sems[ch] = [self.stack.enter_context(nc.semaphore("s_%s_%d" % (ch, j)))
                        for j in range(nsem)]
        per_eng = {e: [] for e in self.ENGS}
        waited = {e: {} for e in self.ENGS}
        for o in ops:
            need = {}
            for d, raw in o.deps.items():
                p = ops[d]
                if not p.needs_inc or self._skip_same(p, o, raw):
                    continue
                if p.chan not in need or need[p.chan] < p.k:
                    need[p.chan] = p.k
            if o.is_dma and o.k > 0:
                if need.get(o.chan, -1) < o.k - 1:
                    need[o.chan] = o.k - 1
            waits = []
            for ch, k in need.items():
                if waited[o.eng].get(ch, -1) >= k:
                    continue
                waited[o.eng][ch] = k
                waits.append((ch, k))
            per_eng[o.eng].append((waits, o))
        finals = {e: {} for e in self.ENGS}
        for eng, oid in self.final_waits:
            ch = ops[oid].chan
            finals[eng][ch] = max(finals[eng].get(ch, -1), ops[oid].k)

        def semval(ch, k):
            mult = 16 if ch.startswith("dma") else 1
            CH = DMA_CHUNK if mult == 16 else SEM_CHUNK
            return sems[ch][k // CH], (k % CH + 1) * mult

        def emit(engname):
            def body(e):
                for waits, o in per_eng[engname]:
                    for ch, k in waits:
                        s, v = semval(ch, k)
                        e.wait_ge(s, v)
                    ins = o.fn(e)
                    if o.needs_inc:
                        s, _ = semval(o.chan, o.k)
                        ins.then_inc(s, 16 if o.is_dma else 1)
                for ch, k in finals[engname].items():
                    s, v = semval(ch, k)
                    e.wait_ge(s, v)
            return body

        allsems = [x for v in sems.values() for x in v]
        for x in allsems:
            nc.gpsimd.sem_clear(x)
        nc.all_engine_barrier()
        with nc.Block() as block:
            if per_eng["sp"] or finals["sp"]:
                block.sync(emit("sp"))
            if per_eng["pe"]:
                block.tensor(emit("pe"))
            if per_eng["act"]:
                block.scalar(emit("act"))
            if per_eng["dve"]:
                block.vector(emit("dve"))
            if per_eng["pool"]:
                block.gpsimd(emit("pool"))
        nc.all_engine_barrier()
        for x in allsems:
            nc.gpsimd.sem_clear(x)
        nc.all_engine_barrier()
        self.stack.close()
        return nc


class Rot:
    def __init__(self, tiles):
        self.tiles = tiles
        self.i = 0

    def next(self):
        t = self.tiles[self.i % len(self.tiles)]
        self.i += 1
        return t


class WStream:
    def __init__(self, P, stage_elems=3072, nbuf=2):
        self.P = P
        self.stage = Rot([P.sbuf("wst%d" % i, [128, stage_elems], F32) for i in range(nbuf)])
        self.wbf = Rot([P.sbuf("wbf%d" % i, [128, stage_elems], BF16) for i in range(nbuf)])
        self.n = 0

    def panel(self, w_ap, kc, c0, ncols):
        P = self.P
        st = self.stage.next()
        wb = self.wbf.next()
        src = w_ap.rearrange("(c p) n -> p c n", p=128)[:, :, c0:c0 + ncols]
        dst = st[:, 0:kc * ncols].rearrange("p (c n) -> p c n", c=kc)
        q = "sp" if self.n % 2 == 0 else "pool"
        self.n += 1
        P.dma(dst, src, writes=[st], q=q)
        P.op("pool", lambda e: e.tensor_copy(out=wb[:, 0:kc * ncols], in_=st[:, 0:kc * ncols]), [st], [wb])
        view = wb[:, 0:kc * ncols].rearrange("p (c n) -> p c n", c=kc)
        return wb, view


def mm_group(P, ps, wb, wv, mi, xs, c0, n):
    nc = P.nc
    kc = len(xs)

    def mm(e):
        ins = None
        with nc.allow_low_precision("bf16 matmul"):
            for c in range(kc):
                ins = e.matmul(ps[:, 0:n], lhsT=wv[:, c, mi * 128:(mi + 1) * 128],
                               rhs=xs[c][:, c0:c0 + n], start=(c == 0), stop=(c == kc - 1))
        return ins
    P.op("pe", mm, [wb] + list(xs), [ps])


def linear_fm(P, ws, pspool, w_ap, n_out, ncols, xs, blocks, evac, col0=0):
    kc = len(xs)
    for p0 in range(0, n_out, ncols):
        wb, wv = ws.panel(w_ap, kc, col0 + p0, ncols)
        for mi in range(ncols // 128):
            m = p0 // 128 + mi
            for bi, (c0, n) in enumerate(blocks):
                ps = pspool.next()
                mm_group(P, ps, wb, wv, mi, xs, c0, n)
                evac(m, bi, ps, c0, n)


def rmsnorm_fm(P, pspool, hs, us, gain_t, ones_t, eps_t, sqpool, rstd_t, blocks, D):
    nch = len(hs)
    for (c0, n) in blocks:
        _rms_block(P, pspool.next(), hs, us, gain_t, ones_t, eps_t, sqpool, rstd_t, c0, n, D, nch)


def _rms_block(P, ps, hs, us, gain_t, ones_t, eps_t, sqpool, rstd_t, c0, n, D, nch):
    for c in range(nch):
        sq = sqpool.next()
        P.op("act", lambda e, sq=sq, c=c: e.activation(out=sq[:, 0:n], in_=hs[c][:, c0:c0 + n], func=AF.Square),
             [hs[c]], [sq])
        P.op("pe", lambda e, sq=sq, c=c: e.matmul(ps[:, 0:n], lhsT=ones_t[:], rhs=sq[:, 0:n],
                                                  start=(c == 0), stop=(c == nch - 1)),
             [sq, ones_t], [ps])
    P.op("act", lambda e: e.activation(out=rstd_t[:, 0:n], in_=ps[:, 0:n], func=AF.Sqrt,
                                       bias=eps_t[:, 0:1], scale=1.0 / D), [ps, eps_t], [rstd_t])
    P.op("dve", lambda e: e.reciprocal(out=rstd_t[:, 0:n], in_=rstd_t[:, 0:n]), [rstd_t], [rstd_t])
    for c in range(nch):
        P.op("dve", lambda e, c=c: e.scalar_tensor_tensor(out=us[c][:, c0:c0 + n], in0=hs[c][:, c0:c0 + n],
                                                          scalar=gain_t[:, c:c + 1], in1=rstd_t[:, 0:n],
                                                          op0=ALU.mult, op1=ALU.mult),
             [hs[c], gain_t, rstd_t], [us[c]])


NT = 1024
NPIECE = 2
HALO = 2
TTOK = NT + HALO
DFF = 3072


def build_F(Dm, dbg=0):
    nc = bass.Bass("TRN2", target_bir_lowering=False)
    kcm = Dm // 128
    hT = nc.dram_tensor("hT", [NPIECE, 1024, TTOK], F32, kind="ExternalInput").ap()
    moT = nc.dram_tensor("moT", [NPIECE, Dm, TTOK], BF16, kind="ExternalInput").ap()
    w_out = nc.dram_tensor("w_out", [Dm, 1024], F32, kind="ExternalInput").ap()
    w_up = nc.dram_tensor("w_up", [1024, 2 * DFF], F32, kind="ExternalInput").ap()
    w_down = nc.dram_tensor("w_down", [DFF, 1024], F32, kind="ExternalInput").ap()
    gain = nc.dram_tensor("gain", [128, 8], F32, kind="ExternalInput").ap()
    cw = nc.dram_tensor("cw", [128, 24, 4], F32, kind="ExternalInput").ap()
    out = nc.dram_tensor("out", [NPIECE, 1024, NT], F32, kind="ExternalOutput").ap()

    P = Prog(nc)
    hs = [P.sbuf("h%d" % c, [128, TTOK], F32) for c in range(8)]
    acts = [P.sbuf("a%d" % c, [128, TTOK], BF16) for c in range(24)]
    us = [P.sbuf("u%d" % c, [128, TTOK], BF16) for c in range(8)]
    gain_sb = P.sbuf("gain_sb", [128, 8], F32)
    cw_sb = P.sbuf("cw_sb", [128, 24, 4], F32)
    ones = P.sbuf("ones", [128, 128], F32)
    eps_t = P.sbuf("eps", [128, 1], F32)
    rstd = P.sbuf("rstd", [128, 512], F32)
    sqpool = Rot([P.sbuf("sq%d" % i, [128, 512], F32) for i in range(2)])
    Gpool = Rot([P.sbuf("G%d" % i, [128, TTOK], F32) for i in range(2)])
    Cpool = Rot([P.sbuf("C%d" % i, [128, NT], F32) for i in range(2)])
    Spool = Rot([P.sbuf("S%d" % i, [128, NT], F32) for i in range(3)])
    hout = Rot([P.sbuf("ho%d" % i, [128, 512], F32) for i in range(2)])
    pspool = Rot([P.psum("ps%d" % i, [128, 512]) for i in range(8)])
    ws = WStream(P, stage_elems=3072, nbuf=3)

    P.dma(gain_sb[:], gain, writes=[gain_sb], q="act")
    P.dma(cw_sb[:], cw, writes=[cw_sb], q="act")
    P.op("pool", lambda e: e.memset(ones[:], 1.0), [], [ones])
    P.op("pool", lambda e: e.memset(eps_t[:], EPS), [], [eps_t])

    main_blocks = [(HALO, 512), (HALO + 512, 512)]
    all_blocks = [(0, HALO)] + main_blocks
    outs = []
    for pc in range(NPIECE):
        for c in range(8):
            P.dma(hs[c][:], hT[pc, c * 128:(c + 1) * 128, :], writes=[hs[c]], q="act")
        for c in range(kcm):
            P.dma(acts[c][:], moT[pc, c * 128:(c + 1) * 128, :], writes=[acts[c]], q="act")

        def ev1(m, bi, ps, c0, n):
            P.op("dve", lambda e: e.tensor_tensor(out=hs[m][:, c0:c0 + n], in0=ps[:, 0:n], in1=hs[m][:, c0:c0 + n],
                                                  op=ALU.add), [ps, hs[m]], [hs[m]])
        linear_fm(P, ws, pspool, w_out, 1024, 128 if kcm > 8 else 256, acts[0:kcm], all_blocks, ev1)
        if dbg == 1:
            for m in range(8):
                outs.append(P.dma(out[pc, m * 128:(m + 1) * 128, :], hs[m][:, HALO:TTOK], reads=[hs[m]], q="sp"))
            continue
        rmsnorm_fm(P, pspool, hs, us, gain_sb, ones, eps_t, sqpool, rstd, all_blocks, 1024.0)
        if dbg == 2:
            for m in range(8):
                for (c0, n) in main_blocks:
                    ho = hout.next()
                    P.op("dve", lambda e, ho=ho, m=m, c0=c0, n=n: e.tensor_copy(out=ho[:, 0:n], in_=us[m][:, c0:c0 + n]), [us[m]], [ho])
                    outs.append(P.dma(out[pc, m * 128:(m + 1) * 128, c0 - HALO:c0 - HALO + n], ho[:, 0:n], reads=[ho], q="sp"))
            continue

        for j0 in range(0, 24, 2):
            Ss = {}
            for half in (0, 1):
                p0 = half * DFF + j0 * 128
                wb, wv = ws.panel(w_up, 8, p0, 256)
                for mi in range(2):
                    j = j0 + mi
                    if half == 0:
                        G = Gpool.next()
                        for (c0, n) in all_blocks:
                            ps = pspool.next()
                            mm_group(P, ps, wb, wv, mi, us, c0, n)
                            P.op("act", lambda e, ps=ps, c0=c0, n=n, G=G: e.copy(out=G[:, c0:c0 + n], in_=ps[:, 0:n]),
                                 [ps], [G])
                        C = Cpool.next()
                        S = Spool.next()
                        Ss[j] = S
                        P.op("act", lambda e, C=C, G=G, j=j: e.activation(out=C[:], in_=G[:, 2:TTOK], func=AF.Identity,
                                                                           bias=cw_sb[:, j, 3:4], scale=cw_sb[:, j, 2:3]),
                             [G, cw_sb], [C])
                        P.op("dve", lambda e, C=C, G=G, j=j: e.scalar_tensor_tensor(
                            out=C[:], in0=G[:, 1:TTOK - 1], scalar=cw_sb[:, j, 1:2], in1=C[:], op0=ALU.mult, op1=ALU.add),
                            [G, cw_sb, C], [C])
                        P.op("dve", lambda e, C=C, G=G, j=j: e.scalar_tensor_tensor(
                            out=C[:], in0=G[:, 0:TTOK - 2], scalar=cw_sb[:, j, 0:1], in1=C[:], op0=ALU.mult, op1=ALU.add),
                            [G, cw_sb, C], [C])
                        P.op("act", lambda e, C=C, S=S: e.activation(out=S[:], in_=C[:], func=AF.Silu), [C], [S])
                    else:
                        S = Ss[j]
                        for (c0, n) in main_blocks:
                            ps = pspool.next()
                            mm_group(P, ps, wb, wv, mi, us, c0, n)
                            P.op("dve", lambda e, ps=ps, c0=c0, n=n, S=S, j=j: e.tensor_tensor(
                                out=acts[j][:, c0:c0 + n], in0=ps[:, 0:n], in1=S[:, c0 - HALO:c0 - HALO + n], op=ALU.mult),
                                [ps, S], [acts[j]])

        def ev4(m, bi, ps, c0, n):
            ho = hout.next()
            P.op("dve", lambda e: e.tensor_tensor(out=ho[:, 0:n], in0=ps[:, 0:n], in1=hs[m][:, c0:c0 + n], op=ALU.add),
                 [ps, hs[m]], [ho])
            outs.append(P.dma(out[pc, m * 128:(m + 1) * 128, c0 - HALO:c0 - HALO + n], ho[:, 0:n], reads=[ho], q="sp"))
        linear_fm(P, ws, pspool, w_down, 1024, 128, acts, main_blocks, ev4)
    for o in outs:
        P.wait_final(o)
    P.build()
    return nc


_NC_CACHE = {}


def _get_nc(key, builder, *args):
    return builder(*args)


def _launch(nc, in_maps):
    res = run_bass_kernel_spmd(nc, in_maps, core_ids=list(range(NCORES)))
    return res.results


def _with_halo(xT, b, t0, n, halo):
    D = xT.shape[1]
    o = np.zeros((D, halo + n), dtype=xT.dtype)
    lo = max(t0 - halo, 0)
    o[:, halo - (t0 - lo):] = xT[b, :, lo:t0 + n]
    return o


def run_F(hT, moT, w_out, w_up, conv_w, conv_b, w_down, gain, dbg=0):
    B, _, S = hT.shape
    Dm = moT.shape[1]
    nc = build_F(Dm, dbg)
    gain_l = np.ascontiguousarray(gain.reshape(8, 128).T)
    cw = np.concatenate([conv_w, conv_b[None, :]], axis=0)
    cw_l = np.ascontiguousarray(cw.reshape(4, 24, 128).transpose(2, 1, 0))
    per = S * B // NCORES
    in_maps = []
    for c in range(NCORES):
        b = (c * per) // S
        t0c = (c * per) % S
        hp = np.stack([_with_halo(hT, b, t0c + p * NT, NT, HALO) for p in range(NPIECE)])
        mp = np.stack([_with_halo(moT, b, t0c + p * NT, NT, HALO) for p in range(NPIECE)])
        in_maps.append({"hT": hp, "moT": mp, "w_out": w_out, "w_up": w_up, "w_down": w_down,
                        "gain": gain_l, "cw": cw_l})
    res = _launch(nc, in_maps)
    out = np.empty_like(hT)
    for c in range(NCORES):
        b = (c * per) // S
        t0c = (c * per) % S
        for p in range(NPIECE):
            out[b, :, t0c + p * NT:t0c + (p + 1) * NT] = res[c]["out"][p]
    return out


import math
SEQ = 8192
TB = 512
C1_2PI = 6.28125
C2_2PI = 2 * math.pi - C1_2PI


def rope_table(P, out_t, pos_f, invf, shift, tt, ki, kf, n):
    P.op("dve", lambda e: e.tensor_scalar(out=out_t[:, 0:n], in0=pos_f[:, 0:n], scalar1=invf[:, 0:1], scalar2=shift,
                                          op0=ALU.mult, op1=ALU.add), [pos_f, invf], [out_t])
    P.op("dve", lambda e: e.tensor_scalar(out=tt[:, 0:n], in0=out_t[:, 0:n], scalar1=1.0 / (2 * math.pi), scalar2=None,
                                          op0=ALU.mult), [out_t], [tt])
    P.op("dve", lambda e: e.tensor_copy(out=ki[:, 0:n], in_=tt[:, 0:n]), [tt], [ki])
    P.op("dve", lambda e: e.tensor_copy(out=kf[:, 0:n], in_=ki[:, 0:n]), [ki], [kf])
    P.op("dve", lambda e: e.scalar_tensor_tensor(out=out_t[:, 0:n], in0=kf[:, 0:n], scalar=-C1_2PI, in1=out_t[:, 0:n],
                                                 op0=ALU.mult, op1=ALU.add), [kf, out_t], [out_t])
    P.op("dve", lambda e: e.scalar_tensor_tensor(out=out_t[:, 0:n], in0=kf[:, 0:n], scalar=-C2_2PI, in1=out_t[:, 0:n],
                                                 op0=ALU.mult, op1=ALU.add), [kf, out_t], [out_t])
    P.op("dve", lambda e: e.tensor_scalar(out=tt[:, 0:n], in0=out_t[:, 0:n], scalar1=math.pi, scalar2=2 * math.pi,
                                          op0=ALU.is_gt, op1=ALU.mult), [out_t], [tt])
    P.op("dve", lambda e: e.tensor_tensor(out=out_t[:, 0:n], in0=out_t[:, 0:n], in1=tt[:, 0:n], op=ALU.subtract),
         [out_t, tt], [out_t])
    P.op("act", lambda e: e.activation(out=out_t[:, 0:n], in_=out_t[:, 0:n], func=AF.Sin), [out_t], [out_t])


def build_A(nsteps=SEQ // TB, dbg=0):
    nc = bass.Bass("TRN2", target_bir_lowering=False)
    hT = nc.dram_tensor("hT", [1024, SEQ], F32, kind="ExternalInput").ap()
    wqkv = nc.dram_tensor("wqkv", [1024, 448], F32, kind="ExternalInput").ap()
    gain = nc.dram_tensor("gain", [128, 8], F32, kind="ExternalInput").ap()
    qkg = nc.dram_tensor("qkg", [128, 2], F32, kind="ExternalInput").ap()
    pos = nc.dram_tensor("pos", [1, SEQ], I32, kind="ExternalInput").ap()
    invf = nc.dram_tensor("invf", [128, 1], F32, kind="ExternalInput").ap()
    rotT = nc.dram_tensor("rotT", [128, 128], F32, kind="ExternalInput").ap()
    onesbd = nc.dram_tensor("onesbd", [128, 128], F32, kind="ExternalInput").ap()
    sinks = nc.dram_tensor("sinks", [64, 4], F32, kind="ExternalInput").ap()
    masks = nc.dram_tensor("masks", [128, 2, 128], F32, kind="ExternalInput").ap()
    oT = nc.dram_tensor("oT", [256, SEQ], BF16, kind="ExternalOutput").ap()

    P = Prog(nc)
    hs = [P.sbuf("h%d" % c, [128, TB], F32) for c in range(8)]
    us = [P.sbuf("u%d" % c, [128, TB], BF16) for c in range(8)]
    wst = P.sbuf("wst", [128, 8, 448], F32)
    wbf = P.sbuf("wbf", [128, 8, 448], BF16)
    gain_sb = P.sbuf("gain_sb", [128, 8], F32)
    qkg_sb = P.sbuf("qkg_sb", [128, 2], F32)
    invf_sb = P.sbuf("invf_sb", [128, 1], F32)
    rot_sb = P.sbuf("rot_sb", [128, 128], F32)
    obd_sb = P.sbuf("obd_sb", [128, 128], F32)
    sink_sb = P.sbuf("sink_sb", [64, 4], F32)
    esink = P.sbuf("esink", [64, 4], F32)
    mask_sb = P.sbuf("mask_sb", [128, 2, 128], F32)
    mask_bf = P.sbuf("mask_bf", [128, 2, 128], BF16)
    ones = P.sbuf("ones", [128, 128], F32)
    ones_bf = P.sbuf("ones_bf", [128, 64], BF16)
    eps_t = P.sbuf("eps", [128, 1], F32)
    rstd = P.sbuf("rstd", [128, TB], F32)
    sqpool = Rot([P.sbuf("sq%d" % i, [128, TB], F32) for i in range(2)])
    posi = P.sbuf("posi", [128, TB], I32)
    posf = P.sbuf("posf", [128, TB], F32)
    cosF = P.sbuf("cosF", [128, TB], F32)
    sinF = P.sbuf("sinF", [128, TB], F32)
    tt = P.sbuf("tt", [128, TB], F32)
    ki = P.sbuf("ki", [128, TB], I32)
    kf = P.sbuf("kf", [128, TB], F32)
    qf = Rot([P.sbuf("qf%d" % i, [128, TB], F32) for i in range(2)])
    qn = Rot([P.sbuf("qn%d" % i, [128, TB], F32) for i in range(2)])
    t1 = Rot([P.sbuf("t1%d" % i, [128, TB], F32) for i in range(2)])
    t2 = Rot([P.sbuf("t2%d" % i, [128, TB], F32) for i in range(2)])
    qT = [P.sbuf("qT%d" % i, [128, TB], BF16) for i in range(2)]
    kTs = [P.sbuf("kT%d" % i, [128, 128 + TB], BF16) for i in range(2)]
    v_sb = P.sbuf("v_sb", [128, 5, 64], BF16)
    Ef = Rot([P.sbuf("Ef%d" % i, [128, 4, 128], BF16) for i in range(4)])
    Eb = Rot([P.sbuf("Eb%d" % i, [128, 4, 128], BF16) for i in range(4)])
    den = Rot([P.sbuf("den%d" % i, [64, 4, 128], F32) for i in range(2)])
    ostep = Rot([P.sbuf("os%d" % i, [64, 4, TB], BF16) for i in range(2)])
    pspool = Rot([P.psum("ps%d" % i, [128, 512]) for i in range(8)])

    for dst, src in ((wst, wqkv.rearrange("(c p) n -> p c n", p=128)), (gain_sb, gain), (qkg_sb, qkg), (invf_sb, invf),
                     (rot_sb, rotT), (obd_sb, onesbd), (sink_sb, sinks), (mask_sb, masks)):
        P.dma(dst[:], src, writes=[dst], q="act")
    P.op("pool", lambda e: e.tensor_copy(out=wbf[:], in_=wst[:]), [wst], [wbf])
    P.op("pool", lambda e: e.tensor_copy(out=mask_bf[:], in_=mask_sb[:]), [mask_sb], [mask_bf])
    P.op("pool", lambda e: e.memset(ones[:], 1.0), [], [ones])
    P.op("pool", lambda e: e.memset(ones_bf[:], 1.0), [], [ones_bf])
    for kT_ in kTs:
        P.op("pool", lambda e, kT_=kT_: e.memset(kT_[:], 0.0), [], [kT_])
    P.op("pool", lambda e: e.memset(eps_t[:], EPS), [], [eps_t])
    P.op("act", lambda e: e.activation(out=esink[:], in_=sink_sb[:], func=AF.Exp), [sink_sb], [esink])

    outs = []
    for s in range(nsteps):
        tok0 = s * TB
        for c in range(8):
            P.dma(hs[c][:], hT[c * 128:(c + 1) * 128, tok0:tok0 + TB], writes=[hs[c]], q="sp")
        P.dma(posi[:], pos[:, tok0:tok0 + TB].partition_broadcast(128), writes=[posi], q="sp")
        rmsnorm_fm(P, pspool, hs, us, gain_sb, ones, eps_t, sqpool, rstd, [(0, TB)], 1024.0)
        P.op("dve", lambda e: e.tensor_copy(out=posf[:], in_=posi[:]), [posi], [posf])
        rope_table(P, sinF, posf, invf_sb, 0.0, tt, ki, kf, TB)
        rope_table(P, cosF, posf, invf_sb, math.pi / 2, tt, ki, kf, TB)
        if s > 0:
            for kT_ in kTs:
                P.op("pool", lambda e, kT_=kT_: e.tensor_copy(out=kT_[:, 0:128], in_=kT_[:, TB:TB + 128]), [kT_], [kT_])
            P.op("pool", lambda e: e.tensor_copy(out=v_sb[:, 0, :], in_=v_sb[:, 4, :]), [v_sb], [v_sb])
        for m in range(3):
            ps = pspool.next()

            def mmq(e, ps=ps, m=m):
                ins = None
                with nc.allow_low_precision("bf16 matmul"):
                    for c in range(8):
                        ins = e.matmul(ps[:, 0:TB], lhsT=wbf[:, c, m * 128:(m + 1) * 128], rhs=us[c][:, 0:TB],
                                       start=(c == 0), stop=(c == 7))
                return ins
            P.op("pe", mmq, [wbf] + us, [ps])
            f = qf.next()
            P.op("act", lambda e, f=f, ps=ps: e.copy(out=f[:], in_=ps[:, 0:TB]), [ps], [f])
            sq = sqpool.next()
            P.op("act", lambda e, f=f, sq=sq: e.activation(out=sq[:], in_=f[:], func=AF.Square), [f], [sq])
            ps2 = pspool.next()
            P.op("pe", lambda e, ps2=ps2, sq=sq: e.matmul(ps2[:, 0:TB], lhsT=obd_sb[:], rhs=sq[:], start=True, stop=True),
                 [obd_sb, sq], [ps2])
            P.op("act", lambda e, ps2=ps2: e.activation(out=rstd[:], in_=ps2[:, 0:TB], func=AF.Sqrt, bias=eps_t[:, 0:1],
                                                        scale=1.0 / 64), [ps2, eps_t], [rstd])
            P.op("dve", lambda e: e.reciprocal(out=rstd[:], in_=rstd[:]), [rstd], [rstd])
            n_ = qn.next()
            gcol = 0 if m < 2 else 1
            P.op("dve", lambda e, n_=n_, f=f, gcol=gcol: e.scalar_tensor_tensor(
                out=n_[:], in0=f[:], scalar=qkg_sb[:, gcol:gcol + 1], in1=rstd[:], op0=ALU.mult, op1=ALU.mult),
                [f, qkg_sb, rstd], [n_])
            ps3 = pspool.next()
            P.op("pe", lambda e, ps3=ps3, n_=n_: e.matmul(ps3[:, 0:TB], lhsT=rot_sb[:], rhs=n_[:], start=True, stop=True),
                 [rot_sb, n_], [ps3])
            a1 = t1.next()
            a2 = t2.next()
            P.op("dve", lambda e, a1=a1, n_=n_: e.tensor_tensor(out=a1[:], in0=n_[:], in1=cosF[:], op=ALU.mult),
                 [n_, cosF], [a1])
            P.op("dve", lambda e, a2=a2, ps3=ps3: e.tensor_tensor(out=a2[:], in0=ps3[:, 0:TB], in1=sinF[:], op=ALU.mult),
                 [ps3, sinF], [a2])
            if m < 2:
                P.op("pool", lambda e, a1=a1, a2=a2, m=m: e.tensor_tensor(out=qT[m][:, 0:TB], in0=a1[:], in1=a2[:], op=ALU.add),
                     [a1, a2], [qT[m]])
            else:
                for hh in range(2):
                    P.op("pool", lambda e, a1=a1, a2=a2, hh=hh: e.tensor_tensor(
                        out=kTs[hh][hh * 64:hh * 64 + 64, 128:128 + TB], in0=a1[hh * 64:hh * 64 + 64, :],
                        in1=a2[hh * 64:hh * 64 + 64, :], op=ALU.add), [a1, a2], [kTs[hh]])
        for tb in range(4):
            ps = pspool.next()

            def mmv(e, ps=ps, tb=tb):
                ins = None
                with nc.allow_low_precision("bf16 matmul"):
                    for c in range(8):
                        ins = e.matmul(ps[:, 0:64], lhsT=us[c][:, tb * 128:(tb + 1) * 128], rhs=wbf[:, c, 384:448],
                                       start=(c == 0), stop=(c == 7))
                return ins
            P.op("pe", mmv, [wbf] + us, [ps])
            P.op("act", lambda e, ps=ps, tb=tb: e.copy(out=v_sb[:, 1 + tb, :], in_=ps[:, 0:64]), [ps], [v_sb])
        osb = ostep.next()
        if dbg == 1:
            for j in range(4):
                P.op("dve", lambda e, j=j, osb=osb: e.tensor_copy(out=osb[:, j, :], in_=qT[j // 2][(j % 2) * 64:(j % 2) * 64 + 64, :]),
                     [qT[0], qT[1]], [osb])
            outs.append(P.dma(oT.rearrange("(j d) t -> d j t", d=64)[:, :, tok0:tok0 + TB], osb[:], reads=[osb], q="sp"))
            continue
        for qb in range(4):
            gblk = s * 4 + qb
            parts = [(1, 0, 128 + qb * 128, 1 + qb)]
            if gblk > 0:
                parts.append((0, 1, qb * 128, qb))
            Es = []
            for (is_cur, mi, kc0, vb) in parts:
                ps = pspool.next()

                def mms(e, ps=ps, kc0=kc0, qb=qb):
                    ins = None
                    with nc.allow_low_precision("bf16 matmul"):
                        for j in range(4):
                            ins = e.matmul(ps[:, j * 128:(j + 1) * 128], lhsT=kTs[j % 2][:, kc0:kc0 + 128],
                                           rhs=qT[j // 2][:, qb * 128:(qb + 1) * 128], start=True, stop=True)
                    return ins
                P.op("pe", mms, [kTs[0], kTs[1], qT[0], qT[1]], [ps])
                ef = Ef.next()
                P.op("act", lambda e, ef=ef, ps=ps: e.activation(out=ef[:].rearrange("p j q -> p (j q)"), in_=ps[:, 0:512],
                                                                  func=AF.Exp, scale=0.125), [ps], [ef])
                eb = Eb.next()
                P.op("dve", lambda e, ef=ef, eb=eb, mi=mi: e.tensor_tensor(
                    out=eb[:], in0=ef[:], in1=mask_bf[:, mi:mi + 1, :].to_broadcast([128, 4, 128]), op=ALU.mult),
                    [ef, mask_bf], [eb])
                Es.append((eb, vb))
            if dbg == 2:
                eb0 = Es[0][0]
                P.op("dve", lambda e, eb0=eb0, osb=osb, qb=qb: e.tensor_copy(out=osb[:, :, qb * 128:(qb + 1) * 128], in_=eb0[0:64, :, :]),
                     [eb0], [osb])
                continue
            pso = pspool.next()
            psd = pspool.next()

            def mmo(e, pso=pso, psd=psd, Es=Es):
                ins = None
                with nc.allow_low_precision("bf16 matmul"):
                    for i, (eb, vb) in enumerate(Es):
                        e.matmul(pso[0:64, 0:512], lhsT=v_sb[:, vb, :], rhs=eb[:].rearrange("p j q -> p (j q)"),
                                 start=(i == 0), stop=(i == len(Es) - 1))
                    for i, (eb, vb) in enumerate(Es):
                        ins = e.matmul(psd[0:64, 0:512], lhsT=ones_bf[:, 0:64], rhs=eb[:].rearrange("p j q -> p (j q)"),
                                       start=(i == 0), stop=(i == len(Es) - 1))
                return ins
            P.op("pe", mmo, [v_sb, ones_bf] + [e_[0] for e_ in Es], [pso, psd])
            dn = den.next()
            P.op("dve", lambda e, dn=dn, psd=psd: e.tensor_tensor(
                out=dn[:], in0=psd[0:64, 0:512].rearrange("p (j q) -> p j q", j=4),
                in1=esink[:, :].unsqueeze(2).to_broadcast([64, 4, 128]), op=ALU.add), [psd, esink], [dn])
            P.op("dve", lambda e, dn=dn: e.reciprocal(out=dn[:], in_=dn[:]), [dn], [dn])
            P.op("dve", lambda e, dn=dn, pso=pso, osb=osb, qb=qb: e.tensor_tensor(
                out=osb[:, :, qb * 128:(qb + 1) * 128], in0=pso[0:64, 0:512].rearrange("p (j q) -> p j q", j=4),
                in1=dn[:], op=ALU.mult), [pso, dn], [osb])
        outs.append(P.dma(oT.rearrange("(j d) t -> d j t", d=64)[:, :, tok0:tok0 + TB], osb[:], reads=[osb], q="sp"))
    for o in outs:
        P.wait_final(o)
    P.build()
    return nc


def attn_consts():
    invf = np.zeros((128, 1), np.float32)
    inv_freq = (500000.0 ** (-np.arange(0, 16, 2, dtype=np.float32) / 16)).astype(np.float32)
    rotT = np.zeros((128, 128), np.float32)
    for p in range(128):
        d = p % 64
        if d < 16:
            invf[p, 0] = inv_freq[d % 8]
        if d < 8:
            rotT[p + 8, p] = -1.0
        elif d < 16:
            rotT[p - 8, p] = 1.0
    onesbd = np.zeros((128, 128), np.float32)
    onesbd[:64, :64] = 1.0
    onesbd[64:, 64:] = 1.0
    j = np.arange(128)[:, None]
    i = np.arange(128)[None, :]
    masks = np.stack([(j <= i), (j > i)], axis=1).astype(np.float32)
    return invf, rotT, onesbd, masks


def run_A(hT, positions, w_in, q_gain, k_gain, sinks, gain, nsteps=SEQ // TB, dbg=0):
    B, _, S = hT.shape
    nc = build_A(nsteps, dbg)
    invf, rotT, onesbd, masks = attn_consts()
    gain_l = np.ascontiguousarray(gain.reshape(8, 128).T)
    qkg = np.stack([np.tile(q_gain, 2), np.tile(k_gain, 2)], axis=1).astype(np.float32)
    in_maps = []
    for c in range(NCORES):
        b, g = c // 4, c % 4
        wq = w_in[:, g * 256:(g + 1) * 256]
        wk = w_in[:, 1024 + g * 64:1024 + (g + 1) * 64]
        wv = w_in[:, 1280 + g * 64:1280 + (g + 1) * 64]
        in_maps.append({"hT": np.ascontiguousarray(hT[b]), "wqkv": np.ascontiguousarray(np.concatenate([wq, wk, wk, wv], axis=1)),
                        "gain": gain_l, "qkg": qkg, "pos": np.ascontiguousarray(positions[b:b + 1]), "invf": invf, "rotT": rotT,
                        "onesbd": onesbd, "sinks": np.ascontiguousarray(np.tile(sinks[g * 4:(g + 1) * 4][None, :], (64, 1))),
                        "masks": masks})
    res = _launch(nc, in_maps)
    out = np.empty((B, 1024, S), dtype=ml_dtypes.bfloat16)
    for c in range(NCORES):
        b, g = c // 4, c % 4
        out[b, g * 256:(g + 1) * 256, :] = res[c]["oT"]
    return out


def TT(P, eng, out, in0, in1, op, R, W):
    P.op(eng, lambda e: e.tensor_tensor(out=out, in0=in0, in1=in1, op=op), R, W)


def TS(P, eng, out, in0, s1, s2, op0, op1, R, W):
    if op1 is None:
        P.op(eng, lambda e: e.tensor_scalar(out=out, in0=in0, scalar1=s1, scalar2=None, op0=op0), R, W)
    else:
        P.op(eng, lambda e: e.tensor_scalar(out=out, in0=in0, scalar1=s1, scalar2=s2, op0=op0, op1=op1), R, W)


def STT(P, eng, out, in0, scalar, in1, op0, op1, R, W):
    P.op(eng, lambda e: e.scalar_tensor_tensor(out=out, in0=in0, scalar=scalar, in1=in1, op0=op0, op1=op1), R, W)


def ACTV(P, out, in_, func, R, W, **kw):
    P.op("act", lambda e: e.activation(out=out, in_=in_, func=func, **kw), R, W)


def CP(P, eng, out, in_, R, W):
    if eng == "act":
        P.op("act", lambda e: e.copy(out=out, in_=in_), R, W)
    else:
        P.op(eng, lambda e: e.tensor_copy(out=out, in_=in_), R, W)


def MMS(P, mms, R, W, lowp=True):
    nc = P.nc

    def f(e):
        ins = None
        if lowp:
            with nc.allow_low_precision("bf16 matmul"):
                for (o, l, r, st, sp) in mms:
                    ins = e.matmul(o, lhsT=l, rhs=r, start=st, stop=sp)
        else:
            for (o, l, r, st, sp) in mms:
                ins = e.matmul(o, lhsT=l, rhs=r, start=st, stop=sp)
        return ins
    P.op("pe", f, R, W)


def TRANSP(P, out, in_, ident, R, W):
    P.op("pe", lambda e: e.transpose(out, in_, ident), R, W)


GW = 1600
NEGBIG = -30000.0


def build_G(nsteps=SEQ // TB, dbg=0, max_ops=None):
    nc = bass.Bass("TRN2", target_bir_lowering=False)
    hT = nc.dram_tensor("hT", [1024, SEQ], F32, kind="ExternalInput").ap()
    wg = nc.dram_tensor("wg", [1024, GW], F32, kind="ExternalInput").ap()
    gain = nc.dram_tensor("gain", [128, 8], F32, kind="ExternalInput").ap()
    convw = nc.dram_tensor("convw", [128, 8, 4], F32, kind="ExternalInput").ap()
    alog = nc.dram_tensor("alog", [128, 4], F32, kind="ExternalInput").ap()
    dtb = nc.dram_tensor("dtb", [128, 4], F32, kind="ExternalInput").ap()
    ngain = nc.dram_tensor("ngain", [128, 1], F32, kind="ExternalInput").ap()
    cst = nc.dram_tensor("cst", [128, 4, 128], F32, kind="ExternalInput").ap()
    negm = nc.dram_tensor("negm", [128, 4, 128], F32, kind="ExternalInput").ap()
    oT = nc.dram_tensor("oT", [512, SEQ], BF16, kind="ExternalOutput").ap()

    P = Prog(nc)
    P.max_ops = max_ops
    sb = P.sbuf
    hs = [sb("h%d" % c, [128, TB], F32) for c in range(8)]
    us = [sb("u%d" % c, [128, TB], BF16) for c in range(8)]
    wst = sb("wst", [128, 8, 200], F32)
    wbf = sb("wbf", [128, 8, GW], BF16)
    gain_sb = sb("gain_sb", [128, 8], F32)
    convw_sb = sb("convw_sb", [128, 8, 4], F32)
    alog_sb = sb("alog_sb", [128, 4], F32)
    dtb_sb = sb("dtb_sb", [128, 4], F32)
    negA = sb("negA", [128, 4], F32)
    ngain_sb = sb("ngain_sb", [128, 1], F32)
    cst_sb = sb("cst_sb", [128, 4, 128], F32)
    negm_sb = sb("negm_sb", [128, 4, 128], F32)
    ident = cst_sb[:, 0, :]
    LinclT = cst_sb[:, 1, :]
    UsT = cst_sb[:, 2, :]
    strictT = cst_sb[:, 3, :]
    ones = sb("ones", [128, 128], F32)
    eps_t = sb("eps", [128, 1], F32)
    one_c = sb("one_c", [128, 1], F32)
    rstd = sb("rstd", [128, TB], F32)
    sqpool = Rot([sb("sq%d" % i, [128, TB], F32) for i in range(2)])
    X = [sb("X%d" % m, [128, 3 + TB], F32) for m in range(8)]
    Yc = Rot([sb("Yc%d" % i, [128, TB], F32) for i in range(2)])
    qkf = [sb("qkf%d" % m, [128, TB], F32) for m in range(4)]
    qTb = [sb("qTb%d" % m, [128, TB], BF16) for m in range(2)]
    kTb = [sb("kTb%d" % m, [128, TB], BF16) for m in range(2)]
    kTf = [sb("kTf%d" % m, [128, TB], F32) for m in range(2)]
    vTf = [sb("vTf%d" % m, [128, TB], F32) for m in range(4)]
    zs = [sb("zs%d" % m, [128, TB], F32) for m in range(4)]
    Oacc = [sb("Oacc%d" % m, [128, TB], F32) for m in range(4)]
    oout = Rot([sb("oout%d" % i, [128, TB], BF16) for i in range(2)])
    ba = sb("ba", [128, 8], F32)
    cols = sb("cols", [128, 8, 4], F32)
    tmp4 = sb("tmp4", [128, 4], F32)
    rhsb = Rot([sb("rhsb%d" % i, [128, 4, 128], F32) for i in range(2)])
    egcb = sb("egcb", [128, 4, 128], F32)
    betab = sb("betab", [128, 4, 128], F32)
    kks = [sb("kks%d" % i, [128, 128], F32) for i in range(2)]
    DT = [sb("DT%d" % h, [128, 128], F32) for h in range(4)]
    Mt = [sb("Mt%d" % h, [128, 128], F32) for h in range(4)]
    QQ = [Rot([sb("QQ%d_%d" % (h, i), [128, 2, 128], F32) for i in range(2)]) for h in range(4)]
    XX = [Rot([sb("XX%d_%d" % (h, i), [128, 128], F32) for i in range(2)]) for h in range(4)]
    TTb = [sb("TTb%d" % h, [128, 128], F32) for h in range(4)]
    kbg = [sb("kbg%d" % h, [128, 128], F32) for h in range(4)]
    kd = [[sb("kd%d_%d" % (h, c), [128, 128], F32) for c in range(2)] for h in range(4)]
    vb = [sb("vb%d" % h, [128, 128], F32) for h in range(4)]
    uw = [sb("uw%d" % h, [128, 256], F32) for h in range(4)]
    qd = [sb("qd%d" % h, [128, 128], F32) for h in range(4)]
    qkm = [sb("qkm%d" % h, [128, 128], F32) for h in range(4)]
    vnew = [sb("vnew%d" % h, [128, 128], F32) for h in range(4)]
    S = [sb("S%d" % h, [128, 128], F32) for h in range(4)]
    psA = Rot([P.psum("psA%d" % i, [128, 512]) for i in range(2)])
    psB = Rot([P.psum("psB%d" % i, [128, 512]) for i in range(5)])
    psO = P.psum("psO", [128, 512])

    q_ = "act"
    for dst, src in ((gain_sb, gain), (convw_sb, convw), (alog_sb, alog), (dtb_sb, dtb), (ngain_sb, ngain),
                     (cst_sb, cst), (negm_sb, negm)):
        P.dma(dst[:], src, writes=[dst], q=q_)
    wsrc = wg.rearrange("(c p) n -> p c n", p=128)
    for c0 in range(0, GW, 200):
        P.dma(wst[:], wsrc[:, :, c0:c0 + 200], writes=[wst], q="sp")
        CP(P, "pool", wbf[:, :, c0:c0 + 200], wst[:], [wst], [wbf])
    P.op("pool", lambda e: e.memset(ones[:], 1.0), [], [ones])
    P.op("pool", lambda e: e.memset(eps_t[:], EPS), [], [eps_t])
    P.op("pool", lambda e: e.memset(one_c[:], 1.0), [], [one_c])
    for t in X:
        P.op("pool", lambda e, t=t: e.memset(t[:, 0:3], 0.0), [], [t])
    for h in range(4):
        for t in (S[h], vnew[h], kd[h][0], kd[h][1]):
            P.op("pool", lambda e, t=t: e.memset(t[:], 0.0), [], [t])
    ACTV(P, negA[:], alog_sb[:], AF.Exp, [alog_sb], [negA])
    TS(P, "dve", negA[:], negA[:], -1.0, None, ALU.mult, None, [negA], [negA])

    outs = []
    for s in range(nsteps):
        tok0 = s * TB
        for c in range(8):
            P.dma(hs[c][:], hT[c * 128:(c + 1) * 128, tok0:tok0 + TB], writes=[hs[c]], q="sp")
        rmsnorm_fm(P, psA, hs, us, gain_sb, ones, eps_t, sqpool, rstd, [(0, TB)], 1024.0)
        for m in range(12):
            ps = psA.next()
            MMS(P, [(ps[:, 0:TB], wbf[:, c, m * 128:(m + 1) * 128], us[c][:, 0:TB], c == 0, c == 7) for c in range(8)],
                [wbf] + us, [ps])
            if m < 8:
                CP(P, "act", X[m][:, 3:3 + TB], ps[:, 0:TB], [ps], [X[m]])
                y = Yc.next()
                ACTV(P, y[:], X[m][:, 3:3 + TB], AF.Identity, [X[m], convw_sb], [y], scale=convw_sb[:, m, 3:4])
                for tap in range(3):
                    STT(P, "dve", y[:], X[m][:, tap:tap + TB], convw_sb[:, m, tap:tap + 1], y[:], ALU.mult, ALU.add,
                        [X[m], convw_sb, y], [y])
                CP(P, "pool", X[m][:, 0:3], X[m][:, TB:TB + 3], [X[m]], [X[m]])
                dst = qkf[m] if m < 4 else vTf[m - 4]
                ACTV(P, dst[:], y[:], AF.Silu, [y], [dst])
            else:
                ACTV(P, zs[m - 8][:], ps[:, 0:TB], AF.Silu, [ps], [zs[m - 8]])
        for m in range(4):
            sq = sqpool.next()
            ACTV(P, sq[:], qkf[m][:], AF.Square, [qkf[m]], [sq])
            ps = psA.next()
            MMS(P, [(ps[:, 0:TB], ones[:], sq[:], True, True)], [ones, sq], [ps], lowp=False)
            ACTV(P, rstd[:], ps[:, 0:TB], AF.Sqrt, [ps, eps_t], [rstd], bias=eps_t[:, 0:1], scale=1.0)
            P.op("dve", lambda e: e.reciprocal(out=rstd[:], in_=rstd[:]), [rstd], [rstd])
            if m < 2:
                STT(P, "dve", qTb[m][:], qkf[m][:], 128.0 ** -0.5, rstd[:], ALU.mult, ALU.mult, [qkf[m], rstd], [qTb[m]])
            else:
                TT(P, "dve", kTf[m - 2][:], qkf[m][:], rstd[:], ALU.mult, [qkf[m], rstd], [kTf[m - 2]])
                CP(P, "pool", kTb[m - 2][:], kTf[m - 2][:], [kTf[m - 2]], [kTb[m - 2]])

        if dbg:
            for h in range(4):
                P.op("pool", lambda e, h=h: e.memset(Oacc[h][:], 1.0), [], [Oacc[h]])
        for sc in range(TB // 128):
            t0 = sc * 128
            tsl = slice(t0, t0 + 128)
            if dbg == 1:
                continue
            ps = psB.next()
            MMS(P, [(ps[:, 0:64], us[c][:, tsl], wbf[:, c, 1536:1600], c == 0, c == 7) for c in range(8)], [wbf] + us, [ps])
            CP(P, "act", ba[:, 0:4], ps[:, 0:4], [ps], [ba])
            CP(P, "act", ba[:, 4:8], ps[:, 32:36], [ps], [ba])
            ACTV(P, cols[:, 0, :], ba[:, 0:4], AF.Sigmoid, [ba], [cols])
            TT(P, "dve", tmp4[:], ba[:, 4:8], dtb_sb[:], ALU.add, [ba, dtb_sb], [tmp4])
            ACTV(P, tmp4[:], tmp4[:], AF.Exp, [tmp4], [tmp4])
            ACTV(P, tmp4[:], tmp4[:], AF.Ln, [tmp4, one_c], [tmp4], bias=one_c[:, 0:1], scale=1.0)
            TT(P, "dve", cols[:, 1, :], tmp4[:], negA[:], ALU.mult, [tmp4, negA], [cols])
            ps = psB.next()
            MMS(P, [(ps[:, 0:4], LinclT, cols[:, 1, :], True, True), (ps[:, 4:8], UsT, cols[:, 1, :], True, True)],
                [cst_sb, cols], [ps], lowp=False)
            CP(P, "act", cols[:, 2:4, :], ps[:, 0:8].rearrange("p (a h) -> p a h", a=2), [ps], [cols])
            ACTV(P, cols[:, 4:6, :], cols[:, 2:4, :], AF.Exp, [cols], [cols])
            TT(P, "dve", cols[:, 6, :], cols[:, 0, :], cols[:, 4, :], ALU.mult, [cols], [cols])
            TS(P, "dve", cols[:, 7, :], cols[:, 2, :], -1.0, None, ALU.mult, None, [cols], [cols])
            r1 = rhsb.next()
            TT(P, "dve", r1[:], cols[:, 1, :].unsqueeze(2).to_broadcast([128, 4, 128]),
               LinclT.unsqueeze(1).to_broadcast([128, 4, 128]), ALU.mult, [cols, cst_sb], [r1])
            psg1 = psA.next()
            MMS(P, [(psg1[:, 0:512], ones[:], r1[:].rearrange("p h i -> p (h i)"), True, True)], [ones, r1], [psg1], lowp=False)
            ACTV(P, egcb[:].rearrange("p h i -> p (h i)"), psg1[:, 0:512], AF.Exp, [psg1], [egcb])
            psg = psA.next()
            MMS(P, [(psg[:, 0:512], ones[:], r1[:].rearrange("p h i -> p (h i)"), True, False),
                    (psg[:, 0:512], ident, negm_sb[:].rearrange("p h i -> p (h i)"), False, True)],
                [ones, r1, cst_sb, negm_sb], [psg], lowp=False)
            r2 = rhsb.next()
            TT(P, "dve", r2[:], cols[:, 0, :].unsqueeze(2).to_broadcast([128, 4, 128]),
               ident.unsqueeze(1).to_broadcast([128, 4, 128]), ALU.mult, [cols, cst_sb], [r2])
            psb = psA.next()
            MMS(P, [(psb[:, 0:512], ones[:], r2[:].rearrange("p h i -> p (h i)"), True, True)], [ones, r2], [psb], lowp=False)
            CP(P, "act", betab[:].rearrange("p h i -> p (h i)"), psb[:, 0:512], [psb], [betab])
            for h in range(4):
                ACTV(P, DT[h][:], psg[:, h * 128:(h + 1) * 128], AF.Exp, [psg, cols], [DT[h]], bias=cols[:, 7, h:h + 1], scale=1.0)
            if dbg == 2:
                continue
            pskq = []
            for qh in range(2):
                ps = psB.next()
                MMS(P, [(ps[:, 0:128], kTb[qh][:, tsl], kTb[qh][:, tsl], True, True),
                        (ps[:, 128:256], kTb[qh][:, tsl], qTb[qh][:, tsl], True, True)], [kTb[qh], qTb[qh]], [ps])
                TT(P, "dve", kks[qh][:], ps[:, 0:128], strictT, ALU.mult, [ps, cst_sb], [kks[qh]])
                pskq.append(ps)
            for h in range(4):
                qh = h // 2
                TT(P, "dve", qkm[h][:], pskq[qh][:, 128:256], DT[h][:], ALU.mult, [pskq[qh], DT[h]], [qkm[h]])
                TT(P, "pool", Mt[h][:], kks[qh][:], DT[h][:], ALU.mult, [kks[qh], DT[h]], [Mt[h]])
                Q = QQ[h].next()
                STT(P, "dve", Q[:, 0, :], Mt[h][:], -1.0, betab[:, h, :], ALU.mult, ALU.mult, [Mt[h], betab], [Q])
                TT(P, "pool", qd[h][:], qTb[qh][:, tsl], egcb[:, h, :], ALU.mult, [qTb[qh], egcb], [qd[h]])
            for qh in range(2):
                ps = psB.next()
                TRANSP(P, ps[:, 0:128], kTf[qh][:, tsl], ident, [kTf[qh], cst_sb], [ps])
                for h in (2 * qh, 2 * qh + 1):
                    TS(P, "act" if False else "dve", kbg[h][:], ps[:, 0:128], cols[:, 6, h:h + 1], None, ALU.mult, None, [ps, cols], [kbg[h]])
                    for c in range(2):
                        TS(P, "pool" if False else "dve", kd[h][c][c * 64:c * 64 + 64, :], ps[c * 64:c * 64 + 64, 0:128],
                           cols[c * 64:c * 64 + 64, 5, h:h + 1], None, ALU.mult, None, [ps, cols], [kd[h][c]])
            for h in range(4):
                ps = psB.next()
                TRANSP(P, ps[:, 0:128], vTf[h][:, tsl], ident, [vTf[h], cst_sb], [ps])
                ACTV(P, vb[h][:], ps[:, 0:128], AF.Identity, [ps, cols], [vb[h]], scale=cols[:, 0, h:h + 1])
            if dbg == 3:
                continue
            Qc = [QQ[h].tiles[(QQ[h].i - 1) % 2] for h in range(4)]
            Xc = [None] * 4
            for h in range(4):
                ps = psB.next()
                TRANSP(P, ps[:, 0:128], Qc[h][:, 0, :], ident, [Qc[h], cst_sb], [ps])
                CP(P, "act", Qc[h][:, 1, :], ps[:, 0:128], [ps], [Qc[h]])
                x = XX[h].next()
                TT(P, "dve", x[:], Qc[h][:, 0, :], ident, ALU.add, [Qc[h], cst_sb], [x])
                Xc[h] = x
            for lvl in range(1, 6):
                for h in range(4):
                    Qo = Qc[h]
                    Qn = QQ[h].next()
                    ps = psB.next()
                    mm = [(ps[:, 128:256], Qo[:, 0, :], Qo[:, 1, :], True, True)]
                    if lvl < 5:
                        mm.append((ps[:, 0:128], Qo[:, 1, :], Qo[:, 0, :], True, True))
                    MMS(P, mm, [Qo], [ps], lowp=False)
                    if lvl < 5:
                        CP(P, "act", Qn[:].rearrange("p a i -> p (a i)"), ps[:, 0:256], [ps], [Qn])
                    else:
                        CP(P, "act", Qn[:, 1, :], ps[:, 128:256], [ps], [Qn])
                    Qc[h] = Qn
                for h in range(4):
                    xo = Xc[h]
                    ps = psB.next()
                    MMS(P, [(ps[:, 0:128], ident, xo[:], True, False), (ps[:, 0:128], Qc[h][:, 1, :], xo[:], False, True)],
                        [cst_sb, Qc[h], xo], [ps], lowp=False)
                    if lvl < 5:
                        xn = XX[h].next()
                        CP(P, "dve", xn[:], ps[:, 0:128], [ps], [xn])
                        Xc[h] = xn
                    else:
                        CP(P, "dve", TTb[h][:], ps[:, 0:128], [ps], [TTb[h]])
            if dbg == 4:
                continue
            for h in range(4):
                ps = psB.next()
                MMS(P, [(ps[:, 0:128], TTb[h][:], vb[h][:], True, True), (ps[:, 128:256], kbg[h][:], TTb[h][:], True, True)],
                    [TTb[h], vb[h], kbg[h]], [ps], lowp=False)
                CP(P, "act", uw[h][:], ps[:, 0:256], [ps], [uw[h]])
            if dbg in (5, 6, 7):
                continue
            for c in range(2):
                r = slice(c * 64, c * 64 + 64)
                for h in range(4):
                    ps = psB.next()
                    MMS(P, [(ps[:, 0:128], uw[h][:, 128:256], S[h][:], True, True)], [uw[h], S[h]], [ps], lowp=False)
                    TT(P, "dve", vnew[h][r, :], uw[h][r, 0:128], ps[r, 0:128], ALU.subtract, [uw[h], ps], [vnew[h]])
                for h in range(4):
                    oc = psO[:, h * 128 + c * 64:h * 128 + c * 64 + 64]
                    MMS(P, [(oc, S[h][:], qd[h][:, r], True, False), (oc, vnew[h][:], qkm[h][:, r], False, True)],
                        [S[h], qd[h], vnew[h], qkm[h]], [psO], lowp=False)
                    ps = psB.next()
                    MMS(P, [(ps[:, 0:128], kd[h][c][:], vnew[h][:], True, True)], [kd[h][c], vnew[h]], [ps], lowp=False)
                    STT(P, "dve", S[h][:], S[h][:], egcb[:, h, c * 64 + 63:c * 64 + 64], ps[:, 0:128], ALU.mult, ALU.add,
                        [S[h], egcb, ps], [S[h]])
            for h in range(4):
                CP(P, "act", Oacc[h][:, tsl], psO[:, h * 128:(h + 1) * 128], [psO], [Oacc[h]])
        for h in range(4):
            sq = sqpool.next()
            ACTV(P, sq[:], Oacc[h][:], AF.Square, [Oacc[h]], [sq])
            ps = psA.next()
            MMS(P, [(ps[:, 0:TB], ones[:], sq[:], True, True)], [ones, sq], [ps], lowp=False)
            ACTV(P, rstd[:], ps[:, 0:TB], AF.Sqrt, [ps, eps_t], [rstd], bias=eps_t[:, 0:1], scale=1.0 / 128)
            P.op("dve", lambda e: e.reciprocal(out=rstd[:], in_=rstd[:]), [rstd], [rstd])
            STT(P, "dve", Oacc[h][:], Oacc[h][:], ngain_sb[:, 0:1], rstd[:], ALU.mult, ALU.mult, [Oacc[h], ngain_sb, rstd], [Oacc[h]])
            oo = oout.next()
            TT(P, "dve", oo[:], Oacc[h][:], zs[h][:], ALU.mult, [Oacc[h], zs[h]], [oo])
            outs.append(P.dma(oT[h * 128:(h + 1) * 128, tok0:tok0 + TB], oo[:], reads=[oo], q="sp"))
    for o in outs:
        P.wait_final(o)
    P.build()
    return nc


def gdn_consts():
    t = np.arange(128)
    same = (t[:, None] // 64) == (t[None, :] // 64)
    ident = np.eye(128, dtype=np.float32)
    LinclT = ((t[:, None] <= t[None, :]) & same).astype(np.float32)
    UsT = ((t[:, None] > t[None, :]) & same).astype(np.float32)
    strictT = ((t[:, None] < t[None, :]) & same).astype(np.float32)
    cst = np.stack([ident, LinclT, UsT, strictT], axis=1)
    neg = np.where((t[:, None] <= t[None, :]) & same, 0.0, NEGBIG).astype(np.float32)
    negm = np.ascontiguousarray(np.tile(neg[:, None, :], (1, 4, 1)))
    return np.ascontiguousarray(cst), negm


def run_G(hT, w_in, conv_w, a_log, dt_bias, norm_gain, gain, nsteps=SEQ // TB, dbg=0, max_ops=None):
    B, _, S = hT.shape
    nc = build_G(nsteps, dbg, max_ops)
    cst, negm = gdn_consts()
    gain_l = np.ascontiguousarray(gain.reshape(8, 128).T)
    in_maps = []
    for c in range(NCORES):
        b, hg = c // 4, c % 4
        qs = slice(hg * 256, (hg + 1) * 256)
        ks = slice(1024 + hg * 256, 1024 + (hg + 1) * 256)
        vs = slice(2048 + hg * 512, 2048 + (hg + 1) * 512)
        zsl = slice(4096 + hg * 512, 4096 + (hg + 1) * 512)
        bs = slice(6144 + hg * 4, 6144 + (hg + 1) * 4)
        as_ = slice(6160 + hg * 4, 6160 + (hg + 1) * 4)
        z28 = np.zeros((1024, 28), np.float32)
        w = np.concatenate([w_in[:, qs], w_in[:, ks], w_in[:, vs], w_in[:, zsl], w_in[:, bs], z28, w_in[:, as_], z28], axis=1)
        cw = np.concatenate([conv_w[:, qs], conv_w[:, ks], conv_w[:, vs]], axis=1)
        cw_l = np.ascontiguousarray(cw.reshape(4, 8, 128).transpose(2, 1, 0))
        in_maps.append({"hT": np.ascontiguousarray(hT[b]), "wg": np.ascontiguousarray(w), "gain": gain_l, "convw": cw_l,
                        "alog": np.ascontiguousarray(np.tile(a_log[hg * 4:(hg + 1) * 4][None, :], (128, 1))),
                        "dtb": np.ascontiguousarray(np.tile(dt_bias[hg * 4:(hg + 1) * 4][None, :], (128, 1))),
                        "ngain": np.ascontiguousarray(norm_gain.reshape(128, 1)), "cst": cst, "negm": negm})
    res = _launch(nc, in_maps)
    out = np.empty((B, 2048, S), dtype=ml_dtypes.bfloat16)
    for c in range(NCORES):
        b, hg = c // 4, c % 4
        out[b, hg * 512:(hg + 1) * 512, :] = res[c]["oT"]
    return out


def kernel(x, positions, mixer_norm, ffn_norm, attn_w_in, attn_q_gain, attn_k_gain, attn_sinks, attn_w_out,
           gdn_w_in, gdn_conv, gdn_a_log, gdn_dt_bias, gdn_norm, gdn_w_out, ffn_w_up, ffn_conv, ffn_conv_b,
           ffn_w_down):
    A = lambda a: np.asarray(a)
    x = A(x).astype(np.float32, copy=False)
    positions = A(positions).astype(np.int32, copy=False)
    (mixer_norm, ffn_norm, attn_w_in, attn_q_gain, attn_k_gain, attn_sinks, attn_w_out, gdn_w_in, gdn_conv, gdn_a_log,
     gdn_dt_bias, gdn_norm, gdn_w_out, ffn_w_up, ffn_conv, ffn_conv_b, ffn_w_down) = [
        A(a).astype(np.float32, copy=False) for a in (
            mixer_norm, ffn_norm, attn_w_in, attn_q_gain, attn_k_gain, attn_sinks, attn_w_out, gdn_w_in, gdn_conv,
            gdn_a_log, gdn_dt_bias, gdn_norm, gdn_w_out, ffn_w_up, ffn_conv, ffn_conv_b, ffn_w_down)]
    hT = np.ascontiguousarray(x.transpose(0, 2, 1))
    depth = mixer_norm.shape[0]
    for i in range(depth):
        j = i // 2
        if i % 2 == 0:
            mo = run_A(hT, positions, attn_w_in[j], attn_q_gain[j], attn_k_gain[j], attn_sinks[j], mixer_norm[i])
            w_out = attn_w_out[j]
        else:
            mo = run_G(hT, gdn_w_in[j], gdn_conv[j], gdn_a_log[j], gdn_dt_bias[j], gdn_norm[j], mixer_norm[i])
            w_out = gdn_w_out[j]
        hT = run_F(hT, mo, np.ascontiguousarray(w_out), np.ascontiguousarray(ffn_w_up[i]), ffn_conv[i], ffn_conv_b[i],
                   np.ascontiguousarray(ffn_w_down[i]), ffn_norm[i])
    return np.ascontiguousarray(hT.transpose(0, 2, 1)).astype(np.float32, copy=False)
```
